# Optimizing a Trainium2 kernel written in Bass

```python
import jax, jax.numpy as jnp
from jax import lax
import numpy as np

D_MODEL = 1024
BATCH = 8
SEQ = 4096
DEPTH = 1

N_MEM = 256
GRID_W = 64
HEAD_DIM = 128
D_MIX = 2 * D_MODEL
A_WIDTH = D_MIX // 2
A_HEADS = A_WIDTH // HEAD_DIM
A_KV_HEADS = A_HEADS // 4
A_GROUP = A_HEADS // A_KV_HEADS
A_KV_WIDTH = A_KV_HEADS * HEAD_DIM
Q_BLOCK = 128
ROPE_AXIS_DIM = HEAD_DIM // 2
ROPE_THETA = 10000.0
B_WIDTH = D_MIX // 4
B_HEADS = B_WIDTH // HEAD_DIM
B_QKV_WIDTH = 3 * B_WIDTH
N_DIR = 2
CONV_K = 5
CHUNK = 64
M_WIDTH = D_MIX // 4
M_HEADS = M_WIDTH // HEAD_DIM
IN_WIDTHS = (A_WIDTH, A_KV_WIDTH, A_KV_WIDTH, B_QKV_WIDTH, N_DIR * B_HEADS, N_DIR * B_HEADS, M_WIDTH, D_MIX)
IN_WIDTH = A_WIDTH + 2 * A_KV_WIDTH + B_QKV_WIDTH + 2 * N_DIR * B_HEADS + M_WIDTH + D_MIX
EPS = 1e-6

kernel_name = "hybrid_parallel_axialgqa_gdeltanet_memxattn"


def rms_norm(x, w):
    x32 = x.astype(jnp.float32)
    y = x32 * lax.rsqrt(jnp.mean(x32 * x32, axis=-1, keepdims=True) + EPS)
    return (y * w.astype(jnp.float32)).astype(x.dtype)


def l2_norm(x):
    x32 = x.astype(jnp.float32)
    return x32 * lax.rsqrt(jnp.sum(x32 * x32, axis=-1, keepdims=True) + EPS)


def split_points():
    pts, acc = [], 0
    for w in IN_WIDTHS[:-1]:
        acc += w
        pts.append(acc)
    return pts


def axial_rope_angles(seq_len):
    rows = seq_len // GRID_W
    row_id = jnp.repeat(jnp.arange(rows), GRID_W)
    col_id = jnp.arange(seq_len) % GRID_W
    inv_freq = ROPE_THETA ** (-jnp.arange(0, ROPE_AXIS_DIM, 2, dtype=jnp.float32) / ROPE_AXIS_DIM)
    ang_row = row_id.astype(jnp.float32)[:, None] * inv_freq
    ang_col = col_id.astype(jnp.float32)[:, None] * inv_freq
    return ang_row, ang_col


def rope_rotate(x, ang):
    cos = jnp.cos(ang)[:, None, :]
    sin = jnp.sin(ang)[:, None, :]
    x1, x2 = jnp.split(x, 2, axis=-1)
    return jnp.concatenate([x1 * cos - x2 * sin, x2 * cos + x1 * sin], axis=-1)


def apply_axial_rope(x, ang_row, ang_col):
    xr, xc = jnp.split(x, 2, axis=-1)
    return jnp.concatenate([rope_rotate(xr, ang_row), rope_rotate(xc, ang_col)], axis=-1).astype(x.dtype)


def axial_gqa_attention(q, k, v):
    b, s = q.shape[:2]
    nblk = s // Q_BLOCK
    qb = (q * HEAD_DIM ** -0.5).reshape(b, nblk, Q_BLOCK, A_KV_HEADS, A_GROUP, HEAD_DIM)
    qb = qb.transpose(1, 0, 3, 4, 2, 5)
    kt = k.transpose(0, 2, 1, 3)
    vt = v.transpose(0, 2, 1, 3)

    def block(qblk):
        sc = jnp.einsum('bkgqd,bksd->bkgqs', qblk, kt).astype(jnp.float32)
        p = jax.nn.softmax(sc, axis=-1).astype(vt.dtype)
        return jnp.einsum('bkgqs,bksd->bkgqd', p, vt)

    o = lax.map(block, qb)
    return o.transpose(1, 0, 4, 2, 3, 5).reshape(b, s, A_WIDTH)


def centred_short_conv(x, w):
    c = x.shape[-1]
    y = lax.conv_general_dilated(
        x, w[:, None, :].astype(x.dtype), window_strides=(1,),
        padding=[(CONV_K // 2, CONV_K // 2)],
        dimension_numbers=('NWC', 'WIO', 'NWC'), feature_group_count=c)
    return jax.nn.silu(y)


def gated_delta_rule_chunked(q, k, v, g, beta):
    *lead, seq_len, dk = q.shape
    dv = v.shape[-1]
    n = seq_len // CHUNK
    nl = len(lead)
    q = (q * dk ** -0.5).reshape(*lead, n, CHUNK, dk)
    k = k.reshape(*lead, n, CHUNK, dk)
    v = v.reshape(*lead, n, CHUNK, dv)
    beta = beta.reshape(*lead, n, CHUNK)
    g = jnp.cumsum(g.reshape(*lead, n, CHUNK), axis=-1)
    incl = jnp.tril(jnp.ones((CHUNK, CHUNK), dtype=bool))
    strict = jnp.tril(jnp.ones((CHUNK, CHUNK), dtype=bool), k=-1)
    decay = jnp.exp(jnp.where(incl, g[..., :, None] - g[..., None, :], -jnp.inf))
    k_beta = k * beta[..., None]
    lower = jnp.where(strict, jnp.einsum('...id,...jd->...ij', k_beta, k) * decay, 0.0)
    t_mat = lower + jnp.eye(CHUNK, dtype=lower.dtype)
    rhs = jnp.concatenate([v * beta[..., None], k_beta * jnp.exp(g)[..., None]], axis=-1)
    sol = lax.linalg.triangular_solve(t_mat, rhs, left_side=True, lower=True, unit_diagonal=True)
    u, w = sol[..., :dv], sol[..., dv:]
    intra = jnp.einsum('...id,...jd->...ij', q, k) * decay
    k_tail = k * jnp.exp(g[..., -1:] - g)[..., None]
    q_head = q * jnp.exp(g)[..., None]
    chunk_decay = jnp.exp(g[..., -1])
    xs = tuple(jnp.moveaxis(t, nl, 0) for t in (q_head, k_tail, u, w, intra, chunk_decay))

    def step(state, inp):
        qc, kc, uc, wc, ac, dc = inp
        v_new = uc - jnp.einsum('...cd,...de->...ce', wc, state)
        o = jnp.einsum('...cd,...de->...ce', qc, state) + jnp.einsum('...ij,...je->...ie', ac, v_new)
        state = state * dc[..., None, None] + jnp.einsum('...cd,...ce->...de', kc, v_new)
        return state, o

    state0 = jnp.zeros((*lead, dk, dv), jnp.float32)
    _, o = lax.scan(step, state0, xs)
    return jnp.moveaxis(o, 0, nl).reshape(*lead, seq_len, dv)


def bidir_gated_deltanet(q, k, v, a, b, a_log, dt_bias):
    g = -jnp.exp(a_log.astype(jnp.float32)) * jax.nn.softplus(a.astype(jnp.float32) + dt_bias.astype(jnp.float32))
    beta = jax.nn.sigmoid(b.astype(jnp.float32))
    g_t = g.transpose(2, 0, 3, 1)
    beta_t = beta.transpose(2, 0, 3, 1)
    g_dir = jnp.stack([g_t[0], jnp.flip(g_t[1], axis=-1)])
    beta_dir = jnp.stack([beta_t[0], jnp.flip(beta_t[1], axis=-1)])

    def both(t):
        t = t.transpose(0, 2, 1, 3)
        return jnp.stack([t, jnp.flip(t, axis=2)])

    o = gated_delta_rule_chunked(both(l2_norm(q)), both(l2_norm(k)), both(v.astype(jnp.float32)), g_dir, beta_dir)
    o = o[0] + jnp.flip(o[1], axis=2)
    return o.transpose(0, 2, 1, 3)


def memory_cross_attention(q, k, v):
    b, s = q.shape[:2]
    sc = jnp.einsum('bshd,bmhd->bhsm', q * HEAD_DIM ** -0.5, k).astype(jnp.float32)
    p = jax.nn.softmax(sc, axis=-1).astype(v.dtype)
    return jnp.einsum('bhsm,bmhd->bshd', p, v).reshape(b, s, M_WIDTH)


def setup_inputs(seed: int = 0) -> dict:
    key = jax.random.key(seed)
    ks = jax.random.split(key, 16)
    f32 = jnp.float32
    nrm = lambda k, shp: jax.random.normal(k, shp, f32)
    dt = jnp.exp(jax.random.uniform(ks[8], (DEPTH, N_DIR, B_HEADS), f32, np.log(1e-3), np.log(1e-1)))
    return {
        "x": nrm(ks[0], (BATCH, SEQ, D_MODEL)),
        "mem": nrm(ks[1], (BATCH, N_MEM, D_MODEL)),
        "norm_pre_w": 1.0 + 0.02 * nrm(ks[2], (DEPTH, D_MODEL)),
        "w_in": nrm(ks[3], (DEPTH, D_MODEL, IN_WIDTH)) * D_MODEL ** -0.5,
        "q_norm_w": 1.0 + 0.02 * nrm(ks[4], (DEPTH, HEAD_DIM)),
        "k_norm_w": 1.0 + 0.02 * nrm(ks[5], (DEPTH, HEAD_DIM)),
        "conv_w": nrm(ks[6], (DEPTH, CONV_K, B_QKV_WIDTH)) * CONV_K ** -0.5,
        "a_log": jnp.log(jax.random.uniform(ks[7], (DEPTH, N_DIR, B_HEADS), f32, 1.0, 16.0)),
        "dt_bias": dt + jnp.log(-jnp.expm1(-dt)),
        "delta_norm_w": 1.0 + 0.02 * nrm(ks[9], (DEPTH, HEAD_DIM)),
        "mem_norm_w": 1.0 + 0.02 * nrm(ks[10], (DEPTH, D_MODEL)),
        "w_mem_kv": nrm(ks[11], (DEPTH, D_MODEL, 2 * M_WIDTH)) * D_MODEL ** -0.5,
        "w_out": nrm(ks[12], (DEPTH, D_MIX, D_MODEL)) * D_MIX ** -0.5,
        "norm_post_w": 1.0 + 0.02 * nrm(ks[13], (DEPTH, D_MODEL)),
    }


def reference(x, mem, norm_pre_w, w_in, q_norm_w, k_norm_w, conv_w, a_log, dt_bias,
              delta_norm_w, mem_norm_w, w_mem_kv, w_out, norm_post_w):
    b, s, _ = x.shape
    n_mem = mem.shape[1]
    ang_row, ang_col = axial_rope_angles(s)
    pts = split_points()
    for l in range(DEPTH):
        h = rms_norm(x, norm_pre_w[l])
        proj = h @ w_in[l]
        aq, ak, av, bqkv, ba, bb, mq, z = jnp.split(proj, pts, axis=-1)

        aq = apply_axial_rope(rms_norm(aq.reshape(b, s, A_HEADS, HEAD_DIM), q_norm_w[l]), ang_row, ang_col)
        ak = apply_axial_rope(rms_norm(ak.reshape(b, s, A_KV_HEADS, HEAD_DIM), k_norm_w[l]), ang_row, ang_col)
        av = av.reshape(b, s, A_KV_HEADS, HEAD_DIM)
        y_a = axial_gqa_attention(aq, ak, av)

        bqkv = centred_short_conv(bqkv, conv_w[l])
        bq, bk, bv = (t.reshape(b, s, B_HEADS, HEAD_DIM) for t in jnp.split(bqkv, 3, axis=-1))
        y_b = bidir_gated_deltanet(bq, bk, bv, ba.reshape(b, s, N_DIR, B_HEADS),
                                   bb.reshape(b, s, N_DIR, B_HEADS), a_log[l], dt_bias[l])
        y_b = rms_norm(y_b, delta_norm_w[l]).reshape(b, s, B_WIDTH).astype(x.dtype)

        mkv = rms_norm(mem, mem_norm_w[l]) @ w_mem_kv[l]
        mk, mv = (t.reshape(b, n_mem, M_HEADS, HEAD_DIM) for t in jnp.split(mkv, 2, axis=-1))
        y_m = memory_cross_attention(mq.reshape(b, s, M_HEADS, HEAD_DIM), mk, mv)

        mix = jnp.concatenate([y_a, y_b, y_m], axis=-1) * jax.nn.silu(z)
        x = x + rms_norm(mix @ w_out[l], norm_post_w[l])
    return x
```

```python
import contextlib
import math
import numpy as np
import ml_dtypes
import concourse.bass as bass
import concourse.mybir as mybir
from concourse.bass_utils import run_bass_kernel_spmd

F32 = mybir.dt.float32
BF16 = mybir.dt.bfloat16
ALU = mybir.AluOpType
AF = mybir.ActivationFunctionType
AX = mybir.AxisListType

D_MODEL = 1024
IN_WIDTH = 5648
N_MEM = 256
EPS = 1e-6
SCALE = 128 ** -0.5
DN_LAG = 0


class Sched:
    NSLOT = 12
    QUEUES = ("sp", "pool", "act")

    def __init__(self, nc, stack):
        self.nc = nc
        self.eng = {"pe": nc.tensor, "act": nc.scalar, "dve": nc.vector,
                    "pool": nc.gpsimd, "sp": nc.sync}
        self.sem = {e: stack.enter_context(nc.semaphore("s_" + e)) for e in self.eng}
        self.sigcount = {e: 0 for e in self.eng}
        self.waited = {e: {f: 0 for f in self.eng} for e in self.eng}
        self.dsem = {q: [stack.enter_context(nc.semaphore("d_%s%d" % (q, i)))
                         for i in range(self.NSLOT)] for q in self.QUEUES}
        self.duses = {q: [0] * self.NSLOT for q in self.QUEUES}
        self.dnext = {q: 0 for q in self.QUEUES}
        self.dwaited = {e: {q: [0] * self.NSLOT for q in self.QUEUES} for e in self.eng}
        self.pending = []
        self.lastw = {}
        self.readers = {}
        self.psum_last = {}
        self.n_inst = 0

    def op(self, eng, fn, reads=(), writes=()):
        self.pending.append(["c", eng, fn, tuple(reads), tuple(writes)])

    def dma(self, queue, fn, reads=(), writes=()):
        self.pending.append(["d", queue, fn, tuple(reads), tuple(writes)])

    def _wait_token(self, e, tok):
        if tok is None:
            return
        E = self.eng[e]
        if tok[0] == "e":
            _, f, sig = tok
            if f == e and e == "pe":
                return
            if self.waited[e][f] >= sig:
                return
            E.wait_ge(self.sem[f], sig)
            self.waited[e][f] = sig
            self.n_inst += 1
        else:
            _, q, slot, val = tok
            if self.dwaited[e][q][slot] >= val:
                return
            E.wait_ge(self.dsem[q][slot], val)
            self.dwaited[e][q][slot] = val
            self.n_inst += 1

    def flush(self):
        ops = self.pending
        self.pending = []
        n = len(ops)
        lastw = dict()
        readers = dict()
        deps_idx = [set() for _ in range(n)]
        deps_tok = [[] for _ in range(n)]
        has_dependents = [False] * n
        pl_batch = {}
        for i, (kind, e, fn, reads, writes) in enumerate(ops):
            for r in reads:
                if r in lastw:
                    deps_idx[i].add(lastw[r])
                elif r in self.lastw:
                    deps_tok[i].append(self.lastw[r])
            for w in writes:
                if w in lastw:
                    deps_idx[i].add(lastw[w])
                    for rd in readers.get(w, ()):
                        deps_idx[i].add(rd)
                else:
                    if w in self.lastw:
                        deps_tok[i].append(self.lastw[w])
                    for rd in readers.get(w, ()):
                        deps_idx[i].add(rd)
                    for rt in self.readers.get(w, ()):
                        deps_tok[i].append(rt)
            if kind == "c":
                for key in tuple(reads) + tuple(writes):
                    if isinstance(key, tuple) and key[0] in ("ps", "acc"):
                        bnk = key[1]
                        cur = pl_batch.setdefault(bnk, {})
                        for f, j in cur.items():
                            if f != e:
                                deps_idx[i].add(j)
                        for f, tk_ in self.psum_last.get(bnk, {}).items():
                            if f != e and f not in cur:
                                deps_tok[i].append(tk_)
                        cur[e] = i
            deps_idx[i].discard(i)
            for r in reads:
                readers.setdefault(r, []).append(i)
            for w in writes:
                lastw[w] = i
                readers[w] = []
                self.readers.pop(w, None)
                self.lastw.pop(w, None)
        for i in range(n):
            best = {}
            keep = set()
            for d in deps_idx[i]:
                if ops[d][0] == "c":
                    ed = ops[d][1]
                    if ed not in best or d > best[ed]:
                        best[ed] = d
                else:
                    keep.add(d)
            deps_idx[i] = keep | set(best.values())
        for i in range(n):
            for d in deps_idx[i]:
                kd, ed = ops[d][0], ops[d][1]
                ki, ei = ops[i][0], ops[i][1]
                if kd == "c" and not (ki == "c" and ei == ed and ed == "pe"):
                    has_dependents[d] = True
        last_on = {}
        for i, o in enumerate(ops):
            if o[0] == "c":
                last_on[o[1]] = i
        for i in last_on.values():
            has_dependents[i] = True
        for r, i in lastw.items():
            has_dependents[i] = True
        for r, lst in readers.items():
            for i in lst:
                has_dependents[i] = True
        for bnk, dct in pl_batch.items():
            for f, i in dct.items():
                has_dependents[i] = True
        tokens = [None] * n
        for i, (kind, e, fn, reads, writes) in enumerate(ops):
            for t in deps_tok[i]:
                self._wait_token(e, t)
            for d in sorted(deps_idx[i]):
                self._wait_token(e, tokens[d])
            if kind == "c":
                ins = fn()
                self.n_inst += 1
                if has_dependents[i]:
                    ins.then_inc(self.sem[e], 1)
                    self.sigcount[e] += 1
                    tokens[i] = ("e", e, self.sigcount[e])
                else:
                    tokens[i] = ("e", e, self.sigcount[e] + 1)
            else:
                q = e
                slot = self.dnext[q]
                self.dnext[q] = (slot + 1) % self.NSLOT
                prev = 16 * self.duses[q][slot]
                if prev > 0 and self.dwaited[q][q][slot] < prev:
                    self.eng[q].wait_ge(self.dsem[q][slot], prev)
                    self.dwaited[q][q][slot] = prev
                    self.n_inst += 1
                ins = fn()
                self.n_inst += 1
                ins.then_inc(self.dsem[q][slot], 16)
                self.duses[q][slot] += 1
                tokens[i] = ("d", q, slot, 16 * self.duses[q][slot])
        for r, i in lastw.items():
            self.lastw[r] = tokens[i]
        for r, lst in readers.items():
            self.readers[r] = [tokens[i] for i in lst]
        for bnk, dct in pl_batch.items():
            pd = self.psum_last.setdefault(bnk, {})
            for f, i in dct.items():
                pd[f] = tokens[i]

    def barrier(self):
        self.flush()
        for e in self.eng:
            for f in self.eng:
                if self.sigcount[f] > self.waited[e][f]:
                    self.eng[e].wait_ge(self.sem[f], self.sigcount[f])
                    self.waited[e][f] = self.sigcount[f]
            for q in self.QUEUES:
                for s in range(self.NSLOT):
                    v = 16 * self.duses[q][s]
                    if v > self.dwaited[e][q][s]:
                        self.eng[e].wait_ge(self.dsem[q][s], v)
                        self.dwaited[e][q][s] = v
        self.lastw = {}
        self.readers = {}
        self.psum_last = {}


def build_nc(SL=4096, dbg=False, phases=(1, 2, 3, 4, 5, 6), grp_limit=None):
    NT = SL // 128
    NB = SL // 512
    nc = bass.Bass("TRN2", target_bir_lowering=False)

    def din(name, shape, dt=F32):
        return nc.dram_tensor(name, shape, dt, kind="ExternalInput").ap()

    x = din("x", [SL, 1024])
    mem = din("mem", [N_MEM, 1024])
    w_in = din("w_in", [1024, IN_WIDTH])
    w_mem = din("w_mem", [1024, 1024])
    w_out = din("w_out", [2048, 1024])
    npre_d = din("npre", [128, 8])
    nmem_d = din("nmem", [128, 8])
    qw_d = din("qw", [128, 128])
    kw_d = din("kw", [128, 128])
    dw_d = din("dw", [128, 128])
    npost_d = din("npost", [128, 1024])
    convw_d = din("convw", [128, 12 * 5])
    alog_d = din("alog", [128, 8])
    dtb_d = din("dtb", [128, 8])
    rope_d = din("rope", [SL, 512])
    cst_d = din("cst", [128, 6 * 128])
    out = nc.dram_tensor("out", [SL, 1024], F32, kind="ExternalOutput").ap()
    sk = "ExternalOutput" if dbg else "Internal"

    def dscr(name, shape, dt=BF16):
        return nc.dram_tensor(name, shape, dt, kind=sk).ap()

    qT_s = dscr("qT_s", [8, 128, SL])
    kT_s = dscr("kT_s", [2, 128, SL])
    v_s = dscr("v_s", [SL, 256])
    bT_s = dscr("bT_s", [12, 128, SL])
    mqT_s = dscr("mqT_s", [4, 128, SL])
    szT_s = dscr("szT_s", [16, 128, SL])
    mixT_s = dscr("mixT_s", [16, 128, SL])
    gates_s = dscr("gates_s", [128, NT * 16], F32)

    with contextlib.ExitStack() as top:
        S = Sched(nc, top)
        E = S.eng
        ps = top.enter_context(nc.psum_tensor("ps", [128, 8, 512], F32))

        def psb(b):
            return ps[:, b, :].bitcast(BF16)

        def T(st, name, shape, dt):
            return st.enter_context(nc.sbuf_tensor("sb_" + name, shape, dt))

        cst = T(top, "cst", [128, 6, 128], F32)
        identb = T(top, "identb", [128, 128], BF16)
        onesb = T(top, "onesb", [128, 128], BF16)
        cb = T(top, "cb", [128, 4], F32)
        gates = T(top, "gates", [128, NT, 16], F32)
        S.dma("sp", lambda: E["sp"].dma_start(out=cst[:], in_=cst_d.rearrange("p (a b) -> p a b", a=6)), writes=["cst"])
        S.op("dve", lambda: E["dve"].tensor_copy(identb[:], cst[:, 0, :]), reads=["cst"], writes=["identb"])
        S.op("dve", lambda: E["dve"].tensor_copy(onesb[:], cst[:, 1, :]), reads=["cst"], writes=["onesb"])
        S.op("dve", lambda: E["dve"].memset(cb[:, 0:1], EPS), writes=["cb"])
        S.op("dve", lambda: E["dve"].memset(cb[:, 1:2], 1.0), writes=["cb"])
        S.op("dve", lambda: E["dve"].memset(cb[:, 2:3], math.log(SCALE)), writes=["cb"])
        S.op("dve", lambda: E["dve"].memset(cb[:, 3:4], 0.0), writes=["cb"])
        S.barrier()

        def mm(out_, lhsT, rhs, start=True, stop=True, r=(), w=(), skip=False):
            if skip:
                S.op("pe", lambda: E["pe"].matmul(out_, lhsT, rhs, start=start, stop=stop, skip_group_check=True), r, w)
            else:
                S.op("pe", lambda: E["pe"].matmul(out_, lhsT, rhs, start=start, stop=stop), r, w)

        def tr(out_, in_, r=(), w=()):
            S.op("pe", lambda: E["pe"].transpose(out_, in_, identb[:]), tuple(r) + ("identb",), w)

        def act(out_, in_, func, r=(), w=(), **kw):
            S.op("act", lambda: E["act"].activation(out=out_, in_=in_, func=func, **kw), r, w)

        def cp(eng, out_, in_, r=(), w=()):
            if eng == "act":
                S.op("act", lambda: E["act"].copy(out_, in_), r, w)
            else:
                S.op(eng, lambda: E[eng].tensor_copy(out_, in_), r, w)

        def tt(eng, out_, a, b, op, r=(), w=()):
            S.op(eng, lambda: E[eng].tensor_tensor(out_, a, b, op), r, w)

        def ts(eng, out_, a, s1, s2, op0, op1=None, r=(), w=()):
            if op1 is None:
                S.op(eng, lambda: E[eng].tensor_scalar(out_, a, s1, s2, op0), r, w)
            else:
                S.op(eng, lambda: E[eng].tensor_scalar(out_, a, s1, s2, op0, op1), r, w)

        def stt(eng, out_, a, sc, b, op0, op1, r=(), w=()):
            S.op(eng, lambda: E[eng].scalar_tensor_tensor(out=out_, in0=a, scalar=sc, in1=b, op0=op0, op1=op1), r, w)

        def ms(eng, out_, val, w=()):
            S.op(eng, lambda: E[eng].memset(out_, val), (), w)

        def dma(q, out_, in_, r=(), w=()):
            S.dma(q, lambda: E[q].dma_start(out=out_, in_=in_), r, w)

        def rstd_ops(src, dst, tmp, scale, key_src, key_dst, key_tmp):
            act(tmp, src, AF.Ln, r=[key_src, "cb"], w=[key_tmp], scale=scale, bias=cb[:, 0:1])
            act(dst, tmp, AF.Exp, r=[key_tmp], w=[key_dst], scale=-0.5)

        def run_pipelined(gens, depth):
            pending = list(gens)
            active = []
            while pending or active:
                if pending and len(active) < depth:
                    active.append(pending.pop(0))
                for g_ in list(active):
                    try:
                        next(g_)
                    except StopIteration:
                        active.remove(g_)

        H = dict(mm=mm, tr=tr, act=act, cp=cp, tt=tt, ts=ts, stt=stt, ms=ms, dma=dma, rstd_ops=rstd_ops)

        if 1 in phases:
            with contextlib.ExitStack() as p12:
                hT = T(p12, "hT", [128, 8, SL], BF16)
                with contextlib.ExitStack() as p1:
                    xt = [T(p1, "xt%d" % i, [128, 1024], F32) for i in range(2)]
                    hb = [T(p1, "hb%d" % i, [128, 1024], BF16) for i in range(2)]
                    junk = T(p1, "junk", [128, 1024], BF16)
                    st1 = T(p1, "st1", [128, NT, 4], F32)
                    ms("dve", st1[:], 0.0, w=["st1"])
                    for t in range(NT):
                        i = t % 2
                        dma("sp", xt[i][:], x[t * 128:(t + 1) * 128, :], w=[("xt", i)])
                        act(junk[:], xt[i][:], AF.Square, r=[("xt", i), "st1"], w=[("st1a", t)],
                            accum_out=st1[:, t, 0:1])
                        rstd_ops(st1[:, t, 0:1], st1[:, t, 2:3], st1[:, t, 1:2], 1.0 / 1024,
                                 ("st1a", t), ("st1c", t), ("st1b", t))
                        ts("dve", hb[i][:], xt[i][:], st1[:, t, 2:3], None, ALU.mult,
                           r=[("xt", i), ("st1c", t)], w=[("hb", i)])
                        bk = 4 + i
                        for kc in range(8):
                            tr(psb(bk)[:, kc * 128:(kc + 1) * 128], hb[i][:, kc * 128:(kc + 1) * 128],
                               r=[("hb", i)], w=[("ps", bk)])
                        cp("act" if t % 2 == 0 else "dve", hT[:, :, t * 128:(t + 1) * 128],
                           psb(bk).rearrange("p (k t) -> p k t", k=8), r=[("ps", bk)], w=["hT"])
                    S.barrier()

                with contextlib.ExitStack() as p2:
                  if 2 in phases:
                      wst1 = T(p2, "wst", [128, 8, 512], F32)
                      wst = [wst1, wst1]
                      wbf = [T(p2, "wbf%d" % i, [128, 8, 512], BF16) for i in range(2)]
                      npre = T(p2, "npre", [128, 8, 1], F32)
                      qwt = T(p2, "qwt", [128, 1, 128], F32)
                      kwt = T(p2, "kwt", [128, 1, 128], F32)
                      cwt = T(p2, "cwt", [128, 12, 5], F32)
                      ropet = [T(p2, "ropet%d" % i, [128, 512], F32) for i in range(4)]
                      sq = [T(p2, "sq%d" % i, [128, 512], F32) for i in range(4)]
                      xn = [T(p2, "xn%d" % i, [128, 512], F32) for i in range(4)]
                      t2 = [T(p2, "t2%d" % i, [128, 512], F32) for i in range(4)]
                      qr = [T(p2, "qr%d" % i, [128, 512], BF16) for i in range(4)]
                      ss4 = [T(p2, "ss4%d" % i, [128, 3, 4, 1], F32) for i in range(4)]
                      qst = [T(p2, "qst%d" % i, [128, 4, 512], BF16) for i in range(2)]
                      vst = [T(p2, "vst%d" % i, [128, 256], BF16) for i in range(2)]
                      fst = [T(p2, "fst%d" % i, [128, 512], BF16) for i in range(4)]
                      preb = [T(p2, "preb%d" % i, [128, SL + 4], BF16) for i in range(2)]
                      dg = [T(p2, "dg%d" % i, [128, 5, 128], BF16) for i in range(2)]
                      sl = [T(p2, "sl%d" % i, [128, 512], F32) for i in range(4)]
                      sqb = [T(p2, "sqb%d" % i, [128, 512], BF16) for i in range(4)]
                      lnv = [T(p2, "lnv%d" % i, [128, 512], F32) for i in range(4)]
                      rsd = [T(p2, "rsd%d" % i, [128, 512], F32) for i in range(4)]

                      dma("pool", npre[:, :, 0], npre_d, w=["npre"])
                      dma("pool", qwt[:, 0, :], qw_d, w=["qwt"])
                      dma("pool", kwt[:, 0, :], kw_d, w=["kwt"])
                      dma("pool", cwt[:], convw_d.rearrange("p (a b) -> p a b", a=12), w=["cwt"])
                      for i in range(2):
                          ms("dve", preb[i][:, 0:2], 0.0, w=["pre_pad"])
                          ms("dve", preb[i][:, SL + 2:SL + 4], 0.0, w=["pre_pad"])
                      for i in range(4):
                          ms("dve", ss4[i][:], 0.0, w=[("ss4a", i), ("ss4b", i), ("ss4c", i)])

                      w_in_v = w_in.rearrange("(kc p) n -> p kc n", p=128)
                      groups = [("A", 0, 512, 0), ("A", 512, 512, 4), ("KV", 1024, 512, 0), ("G", 3072, 16, 0)]
                      groups += [("B", 1536 + 512 * i, 512, 4 * i) for i in range(3)]
                      groups += [("M", 3088, 512, 0)]
                      groups += [("Z", 3600 + 512 * i, 512, 4 * i) for i in range(4)]
                      cnt = {"bank": 0, "pp": 0, "f": 0, "bb": 0}

                      def nbank():
                          b = cnt["bank"] % 4
                          cnt["bank"] += 1
                          return b

                      def load_w(gi):
                          kind, c0, n, _ = groups[gi]
                          wb = gi % 2
                          dma("pool", wst[wb][:, :, 0:n], w_in_v[:, :, c0:c0 + n], w=["wst"])
                          tt("pool", wbf[wb][:, :, 0:n], wst[wb][:, :, 0:n], npre[:].to_broadcast([128, 8, n]), ALU.mult,
                             r=["wst", "npre"], w=[("wbf", wb)])

                      def qk_post(t, b, nh, wt, wkey, dst, h0):
                          i = cnt["pp"] % 4
                          cnt["pp"] += 1
                          W = nh * 128
                          pin = ps[:, b, 0:W]
                          hv = lambda a: a.rearrange("p (h d) -> p h d", h=nh)
                          dma("sp", ropet[i][:], rope_d[t * 128:(t + 1) * 128, :], w=[("ropet", i)])
                          for hh in range(nh):
                              act(sq[i][:, hh * 128:(hh + 1) * 128], pin[:, hh * 128:(hh + 1) * 128], AF.Square,
                                  r=[("ps", b), ("ss4a", i)], w=[("sq", i), ("ss4a", i)], accum_out=ss4[i][:, 0, hh, :])
                          yield
                          rstd_ops(ss4[i][:, 0, 0:nh, 0], ss4[i][:, 2, 0:nh, 0], ss4[i][:, 1, 0:nh, 0], 1.0 / 128,
                                   ("ss4a", i), ("ss4c", i), ("ss4b", i))
                          yield
                          tt("dve", hv(xn[i][:, 0:W]), hv(pin), ss4[i][:, 2, 0:nh, :].to_broadcast([128, nh, 128]), ALU.mult,
                             r=[("ps", b), ("ss4c", i)], w=[("xn", i)])
                          tt("dve", hv(xn[i][:, 0:W]), hv(xn[i][:, 0:W]), wt[:].to_broadcast([128, nh, 128]), ALU.mult,
                             r=[("xn", i), wkey], w=[("xn", i)])
                          yield
                          HA = nh * 2
                          rv = lambda a: a.rearrange("p (a s m) -> p a s m", a=HA, s=2)
                          xv, t1v, t2v, ov = rv(xn[i][:, 0:W]), rv(sq[i][:, 0:W]), rv(t2[i][:, 0:W]), rv(qr[i][:, 0:W])
                          cosv = ropet[i][:, 0:256].rearrange("p (a o m) -> p a o m", a=8, o=1)[:, 0:HA]
                          sinv = ropet[i][:, 256:512].rearrange("p (a m) -> p a m", a=8)[:, 0:HA]
                          tt("pool", t1v, xv, cosv.to_broadcast([128, HA, 2, 32]), ALU.mult,
                             r=[("xn", i), ("ropet", i)], w=[("sq", i)])
                          tt("dve", t2v[:, :, 0, :], xv[:, :, 1, :], sinv, ALU.mult, r=[("xn", i), ("ropet", i)], w=[("t2", i)])
                          tt("dve", t2v[:, :, 1, :], xv[:, :, 0, :], sinv, ALU.mult, r=[("xn", i), ("ropet", i)], w=[("t2", i)])
                          yield
                          tt("dve", ov[:, :, 0, :], t1v[:, :, 0, :], t2v[:, :, 0, :], ALU.subtract,
                             r=[("sq", i), ("t2", i)], w=[("qr", i)])
                          tt("dve", ov[:, :, 1, :], t1v[:, :, 1, :], t2v[:, :, 1, :], ALU.add,
                             r=[("sq", i), ("t2", i)], w=[("qr", i)])
                          yield
                          bk = 4 + i % 2
                          for hh in range(nh):
                              tr(psb(bk)[:, hh * 128:(hh + 1) * 128], qr[i][:, hh * 128:(hh + 1) * 128],
                                 r=[("qr", i)], w=[("ps", bk)])
                          si = (t // 4) % 2
                          cp("act", qst[si][:, 0:nh, (t % 4) * 128:(t % 4 + 1) * 128],
                             psb(bk)[:, 0:W].rearrange("p (h t) -> p h t", h=nh), r=[("ps", bk)], w=[("qst", si)])
                          if t % 4 == 3:
                              tb = t // 4
                              dma("sp", dst[h0:h0 + nh].rearrange("h d t -> d h t")[:, :, tb * 512:(tb + 1) * 512],
                                  qst[si][:, 0:nh, :], r=[("qst", si)], w=[])

                      if grp_limit is not None:
                          groups[:] = [groups[k] for k in grp_limit]
                      load_w(0)
                      for gi, (kind, c0, n, i0) in enumerate(groups):
                          if gi + 1 < len(groups):
                              load_w(gi + 1)
                          wb = gi % 2
                          if kind in ("A", "KV", "G"):
                              def tile_task(t, kind=kind, n=n, wb=wb, i0=i0):
                                  b = nbank()
                                  for kc in range(8):
                                      mm(ps[:, b, 0:n], hT[:, kc, t * 128:(t + 1) * 128], wbf[wb][:, kc, 0:n],
                                         start=(kc == 0), stop=(kc == 7), r=["hT", ("wbf", wb)], w=[("ps", b)])
                                  yield
                                  if kind == "A":
                                      yield from qk_post(t, b, 4, qwt, "qwt", qT_s, i0)
                                  elif kind == "KV":
                                      vi = t % 2
                                      cp("act", vst[vi][:], ps[:, b, 256:512], r=[("ps", b)], w=[("vst", vi)])
                                      dma("sp", v_s[t * 128:(t + 1) * 128, :], vst[vi][:], r=[("vst", vi)], w=[])
                                      yield from qk_post(t, b, 2, kwt, "kwt", kT_s, 0)
                                  else:
                                      cp("act", gates[:, t, :], ps[:, b, 0:16], r=[("ps", b)], w=["gates"])
                              run_pipelined([tile_task(t) for t in range(NT)], 4)
                          elif kind in ("M", "Z"):
                              for cl in range(4):
                                  ct = i0 + cl
                                  for tb in range(NB):
                                      b = nbank()
                                      for kc in range(8):
                                          mm(ps[:, b, :], wbf[wb][:, kc, cl * 128:(cl + 1) * 128],
                                             hT[:, kc, tb * 512:(tb + 1) * 512], start=(kc == 0), stop=(kc == 7),
                                             r=["hT", ("wbf", wb)], w=[("ps", b)])
                                      fi = cnt["f"] % 2
                                      cnt["f"] += 1
                                      dst = mqT_s if kind == "M" else szT_s
                                      act(fst[fi][:], ps[:, b, :], AF.Copy if kind == "M" else AF.Silu,
                                          r=[("ps", b)], w=[("fst", fi)])
                                      dma("sp", dst[ct, :, tb * 512:(tb + 1) * 512], fst[fi][:], r=[("fst", fi)], w=[])
                          else:
                              pre_owner = {}
                              pcnt = {"b": 0, "c": 0}

                              def projA(cl, wb=wb, i0=i0):
                                  ct = i0 + cl
                                  pbi = ct % 2
                                  tt("dve", dg[pbi][:], identb[:].unsqueeze(1).to_broadcast([128, 5, 128]),
                                     cwt[:, ct, :].unsqueeze(2).to_broadcast([128, 5, 128]), ALU.mult,
                                     r=["identb", "cwt"], w=[("dg", pbi)])
                                  for tb in range(NB):
                                      b = pcnt["b"] % 2
                                      pcnt["b"] += 1
                                      for kc in range(8):
                                          mm(ps[:, b, :], wbf[wb][:, kc, cl * 128:(cl + 1) * 128],
                                             hT[:, kc, tb * 512:(tb + 1) * 512], start=(kc == 0), stop=(kc == 7),
                                             r=["hT", ("wbf", wb)], w=[("ps", b)])
                                      cp("dve", preb[pbi][:, 2 + tb * 512:2 + (tb + 1) * 512], ps[:, b, :],
                                         r=[("ps", b), "pre_pad"], w=[("pre", pbi, tb)])
                                      pre_owner[(pbi, tb)] = ct
                                      yield

                              def convB(cl, tb, i0=i0):
                                  ct = i0 + cl
                                  pbi = ct % 2
                                  o = tb * 512
                                  fi = pcnt["c"] % 4
                                  pcnt["c"] += 1
                                  bk = 2 + fi
                                  nbrs = list(range(max(0, tb - 1), min(NB, tb + 2)))
                                  for k in nbrs:
                                      assert pre_owner.get((pbi, k)) == ct, "conv block scheduled before its projection"
                                  prd = [("pre", pbi, k) for k in nbrs] + ["pre_pad", ("dg", pbi)]
                                  for j in range(5):
                                      mm(ps[:, bk, :], dg[pbi][:, j, :], preb[pbi][:, o + j:o + j + 512],
                                         start=(j == 0), stop=(j == 4), r=prd, w=[("ps", bk)])
                                  yield
                                  if ct >= 8:
                                      act(fst[fi][:], ps[:, bk, :], AF.Silu, r=[("ps", bk)], w=[("fst", fi)])
                                  else:
                                      act(lnv[fi][:], ps[:, bk, :], AF.Exp, r=[("ps", bk)], w=[("lnv", fi)], scale=-1.0)
                                      act(lnv[fi][:], lnv[fi][:], AF.Ln, r=[("lnv", fi), "cb"], w=[("lnv", fi)], bias=cb[:, 1:2])
                                      act(rsd[fi][:], lnv[fi][:], AF.Exp, r=[("lnv", fi)], w=[("rsd", fi)], scale=-1.0)
                                      yield
                                      tt("dve", sl[fi][:], ps[:, bk, :], rsd[fi][:], ALU.mult,
                                         r=[("ps", bk), ("rsd", fi)], w=[("sl", fi)])
                                      tt("pool", sqb[fi][:], sl[fi][:], sl[fi][:], ALU.mult, r=[("sl", fi)], w=[("sqb", fi)])
                                      yield
                                      b2 = 6 + fi % 2
                                      mm(ps[:, b2, :], onesb[:], sqb[fi][:], r=[("sqb", fi), "onesb"], w=[("ps", b2)])
                                      act(lnv[fi][:], ps[:, b2, :], AF.Ln, r=[("ps", b2), "cb"], w=[("lnv", fi)], bias=cb[:, 0:1])
                                      act(rsd[fi][:], lnv[fi][:], AF.Exp, r=[("lnv", fi)], w=[("rsd", fi)], scale=-0.5)
                                      yield
                                      tt("dve", fst[fi][:], sl[fi][:], rsd[fi][:], ALU.mult,
                                         r=[("sl", fi), ("rsd", fi)], w=[("fst", fi)])
                                  dma("sp", bT_s[ct, :, o:o + 512], fst[fi][:], r=[("fst", fi)], w=[])

                              tasks = [projA(0), projA(1)] + [convB(0, tb) for tb in range(NB)] + [projA(2)]
                              tasks += [convB(1, tb) for tb in range(NB)] + [projA(3)]
                              tasks += [convB(2, tb) for tb in range(NB)] + [convB(3, tb) for tb in range(NB)]
                              run_pipelined(tasks, 4)
                      if dbg:
                          dma("sp", gates_s, gates[:].rearrange("p t c -> p (t c)"), r=["gates"], w=["gates_s"])
                      S.barrier()

        def attn_finish(pa, ob, sbk, yst, dst_rows, qb, tag):
            S.op("dve", lambda: E["dve"].reciprocal(pa["rec"][:], ps[:, sbk, :]), [("ps", sbk)], ["rec"])
            tt("dve", yst[:], ps[:, ob, :], pa["rec"][:], ALU.mult, r=[("ps", ob), "rec"], w=[tag])
            dma("pool", dst_rows[:, qb * 512:(qb + 1) * 512], yst[:], r=[tag], w=[])

        if 3 in phases:
            with contextlib.ExitStack() as p3:
                kT = T(p3, "kT", [128, 2, SL], BF16)
                vp = T(p3, "vp", [128, NT, 2, 130], BF16)
                qt = [T(p3, "qt%d" % i, [128, 2, 512], BF16) for i in range(2)]
                pT = [T(p3, "pT%d" % i, [128, 1024], BF16) for i in range(3)]
                asum = T(p3, "asum", [128, 512], F32)
                oc = [T(p3, "oc%d" % i, [128, 512], F32) for i in range(2)]
                rec2 = [T(p3, "rec2%d" % i, [128, 512], F32) for i in range(2)]
                pa = {"rec": T(p3, "rec", [128, 512], F32)}
                yst = [T(p3, "yst%d" % i, [128, 512], BF16) for i in range(2)]
                dma("sp", kT[:], kT_s.rearrange("g d t -> d g t"), w=["kT"])
                ms("dve", vp[:], 1.0, w=["vp"])
                for g in range(2):
                    dma("pool", vp[:, :, g, 0:128], v_s[:, g * 128:(g + 1) * 128].rearrange("(t p) d -> p t d", p=128),
                        r=["vp"], w=["vp"])
                step = 0
                blk = 0
                for g in range(2):
                    for qb in range(NB):
                        for hp in range(2):
                            h0 = g * 4 + hp * 2
                            qi = blk % 2
                            blk += 1
                            dma("sp", qt[qi][:], qT_s[h0:h0 + 2].rearrange("h d t -> d h t")[:, :, qb * 512:(qb + 1) * 512],
                                w=[("qt", qi)])
                            def issue_s(kt, pb):
                                for j in range(2):
                                    mm(ps[:, pb + j, :], kT[:, g, kt * 128:(kt + 1) * 128], qt[qi][:, j, :],
                                       r=["kT", ("qt", qi)], w=[("ps", pb)])

                            issue_s(0, 2 * (step % 2))
                            for kt in range(NT):
                                pb = 2 * (step % 2)
                                pi = step % 3
                                step += 1
                                if kt + 1 < NT:
                                    issue_s(kt + 1, 2 * (step % 2))
                                act(pT[pi][:], ps[:, pb:pb + 2, :].rearrange("p a b -> p (a b)"), AF.Exp,
                                    r=[("ps", pb)], w=[("pT", pi)], scale=SCALE)
                                for j in range(2):
                                    mm(ps[:, 4 + j, :], vp[:, kt, g, 0:128], pT[pi][:, j * 512:(j + 1) * 512],
                                       start=(kt == 0), stop=(kt == NT - 1), r=[("pT", pi), "vp"], w=[("ps", 4 + j)])
                                mm(ps[:, 6, :], onesb[:], pT[pi][:, 0:512], start=(kt == 0), stop=(kt == NT - 1),
                                   r=[("pT", pi), "onesb"], w=[("ps", 6)])
                                if kt == 0:
                                    cp("dve", asum[:], pT[pi][:, 512:1024], r=[("pT", pi)], w=["asum"])
                                else:
                                    tt("dve", asum[:], asum[:], pT[pi][:, 512:1024], ALU.add, r=[("pT", pi), "asum"], w=["asum"])
                            for j in range(2):
                                cp("act", oc[j][:], ps[:, 4 + j, :], r=[("ps", 4 + j)], w=[("oc", j)])
                            mm(ps[:, 7, :], cst[:, 1, :], asum[:], r=["asum", "cst"], w=[("ps", 7)])
                            for j in range(2):
                                S.op("dve", lambda j=j: E["dve"].reciprocal(rec2[j][:], ps[:, 6 + j, :]), [("ps", 6 + j)], [("rec2", j)])
                                tt("dve", yst[j][:], oc[j][:], rec2[j][:], ALU.mult, r=[("oc", j), ("rec2", j)], w=[("yst", j)])
                                dma("pool", mixT_s[h0 + j][:, qb * 512:(qb + 1) * 512], yst[j][:], r=[("yst", j)], w=[])
                S.barrier()

        if 4 in phases:
            with contextlib.ExitStack() as p4:
                mt = [T(p4, "mt%d" % i, [128, 1024], F32) for i in range(2)]
                mb = [T(p4, "mb%d" % i, [128, 1024], BF16) for i in range(2)]
                junk = T(p4, "junk4", [128, 1024], BF16)
                st4 = T(p4, "st4", [128, 2, 4], F32)
                hmT = T(p4, "hmT", [128, 8, 256], BF16)
                wst = [T(p4, "wst4%d" % i, [128, 8, 512], F32) for i in range(2)]
                wmb = [T(p4, "wmb%d" % i, [128, 8, 512], BF16) for i in range(2)]
                nmem = T(p4, "nmem", [128, 8, 1], F32)
                mkT = T(p4, "mkT", [128, 4, 256], BF16)
                mvp = T(p4, "mvp", [128, 2, 4, 130], BF16)
                mq = [T(p4, "mq%d" % i, [128, 512], BF16) for i in range(2)]
                pT = [T(p4, "pT4%d" % i, [128, 512], BF16) for i in range(4)]
                pa = {"rec": T(p4, "rec4", [128, 512], F32)}
                yst = [T(p4, "yst4%d" % i, [128, 512], BF16) for i in range(2)]
                dma("pool", nmem[:, :, 0], nmem_d, w=["nmem"])
                ms("dve", st4[:], 0.0, w=["st4"])
                ms("dve", mvp[:], 1.0, w=["mvp"])
                w_mem_v = w_mem.rearrange("(kc p) n -> p kc n", p=128)
                for i in range(2):
                    dma("pool", wst[i][:], w_mem_v[:, :, i * 512:(i + 1) * 512], w=[("wst4", i)])
                    tt("pool", wmb[i][:], wst[i][:], nmem[:].to_broadcast([128, 8, 512]), ALU.mult,
                       r=[("wst4", i), "nmem"], w=[("wmb", i)])
                for t in range(2):
                    dma("sp", mt[t][:], mem[t * 128:(t + 1) * 128, :], w=[("mt", t)])
                    act(junk[:], mt[t][:], AF.Square, r=[("mt", t), "st4"], w=[("st4a", t)],
                        accum_out=st4[:, t, 0:1])
                    rstd_ops(st4[:, t, 0:1], st4[:, t, 2:3], st4[:, t, 1:2], 1.0 / 1024, ("st4a", t), ("st4c", t), ("st4b", t))
                    ts("dve", mb[t][:], mt[t][:], st4[:, t, 2:3], None, ALU.mult, r=[("mt", t), ("st4c", t)], w=[("mb", t)])
                    for kc in range(8):
                        tr(psb(t)[:, kc * 128:(kc + 1) * 128], mb[t][:, kc * 128:(kc + 1) * 128], r=[("mb", t)], w=[("ps", t)])
                    cp("act", hmT[:, :, t * 128:(t + 1) * 128], psb(t).rearrange("p (k t) -> p k t", k=8),
                       r=[("ps", t)], w=["hmT"])
                for hh in range(4):
                    b = 2 + hh % 2
                    for kc in range(8):
                        mm(ps[:, b, 0:256], wmb[0][:, kc, hh * 128:(hh + 1) * 128], hmT[:, kc, :],
                           start=(kc == 0), stop=(kc == 7), r=[("wmb", 0), "hmT"], w=[("ps", b)])
                    cp("act", mkT[:, hh, :], ps[:, b, 0:256], r=[("ps", b)], w=["mkT"])
                for kt in range(2):
                    b = 2 + kt
                    for kc in range(8):
                        mm(ps[:, b, :], hmT[:, kc, kt * 128:(kt + 1) * 128], wmb[1][:, kc, :],
                           start=(kc == 0), stop=(kc == 7), r=[("wmb", 1), "hmT"], w=[("ps", b)])
                    cp("act", mvp[:, kt, :, 0:128], ps[:, b, :].rearrange("p (h d) -> p h d", h=4),
                       r=[("ps", b), "mvp"], w=["mvp"])
                step = 0
                blk = 0
                for hh in range(4):
                    for qb in range(NB):
                        qi = blk % 2
                        blk += 1
                        dma("sp", mq[qi][:], mqT_s[hh, :, qb * 512:(qb + 1) * 512], w=[("mq", qi)])
                        ob = 4 + blk % 2
                        sbk = 6 + blk % 2
                        for kt in range(2):
                            sb = step % 4
                            pi = step % 4
                            step += 1
                            mm(ps[:, sb, :], mkT[:, hh, kt * 128:(kt + 1) * 128], mq[qi][:],
                               r=["mkT", ("mq", qi)], w=[("ps", sb)])
                            act(pT[pi][:], ps[:, sb, :], AF.Exp, r=[("ps", sb)], w=[("pT", pi)], scale=SCALE)
                            mm(ps[:, ob, :], mvp[:, kt, hh, 0:128], pT[pi][:], start=(kt == 0), stop=(kt == 1),
                               r=[("pT", pi), "mvp"], w=[("ps", ob)])
                            mm(ps[:, sbk, :], onesb[:], pT[pi][:], start=(kt == 0), stop=(kt == 1),
                               r=[("pT", pi), "onesb"], w=[("ps", sbk)])
                        yi = blk % 2
                        attn_finish(pa, ob, sbk, yst[yi], mixT_s[12 + hh], qb, ("yst", yi))
                S.barrier()

        if 5 in phases:
            deltanet_phase(nc, S, E, T, H, ps, psb, cst, identb, cb, gates, bT_s, mixT_s, alog_d, dtb_d, dw_d, SL)

        if 6 in phases:
            with contextlib.ExitStack() as p6:
                wo = T(p6, "wo", [128, 16, 1024], BF16)
                wst = [T(p6, "wst6%d" % i, [128, 4, 1024], F32) for i in range(2)]
                npost = T(p6, "npost", [128, 1024], F32)
                mx = [T(p6, "mx%d" % i, [128, 16, 512], BF16) for i in range(2)]
                sz = [T(p6, "sz%d" % i, [128, 16, 512], BF16) for i in range(2)]
                xt = [T(p6, "xt6%d" % i, [128, 1024], F32) for i in range(2)]
                yt = [T(p6, "yt%d" % i, [128, 1024], F32) for i in range(2)]
                junk = T(p6, "junk6", [128, 1024], BF16)
                st6 = T(p6, "st6", [128, NT, 4], F32)
                dma("pool", npost[:], npost_d, w=["npost"])
                ms("dve", st6[:], 0.0, w=["st6"])
                w_out_v = w_out.rearrange("(kc p) n -> p kc n", p=128)
                for c in range(4):
                    i = c % 2
                    dma("pool", wst[i][:], w_out_v[:, c * 4:(c + 1) * 4, :], w=[("wst6", i)])
                    cp("pool", wo[:, c * 4:(c + 1) * 4, :], wst[i][:], r=[("wst6", i)], w=["wo"])
                mixv = mixT_s.rearrange("c p t -> p c t")
                szv = szT_s.rearrange("c p t -> p c t")
                def load_gate(qb):
                    i = qb % 2
                    dma("sp", mx[i][:], mixv[:, :, qb * 512:(qb + 1) * 512], w=[("mx", i)])
                    dma("sp", sz[i][:], szv[:, :, qb * 512:(qb + 1) * 512], w=[("sz", i)])
                    for q4 in range(4):
                        tt("dve", mx[i][:, q4 * 4:(q4 + 1) * 4, :], mx[i][:, q4 * 4:(q4 + 1) * 4, :], sz[i][:, q4 * 4:(q4 + 1) * 4, :],
                           ALU.mult, r=[("mx", i), ("sz", i)], w=[("mx", i)])

                load_gate(0)
                for qb in range(NB):
                    i = qb % 2
                    if qb + 1 < NB:
                        load_gate(qb + 1)
                    for sub in range(4):
                        t = qb * 4 + sub
                        xi = t % 2
                        b0 = (t % 4) * 2
                        dma("sp", xt[xi][:], x[t * 128:(t + 1) * 128, :], w=[("xt6", xi)])
                        for hf in range(2):
                            for kc in range(16):
                                mm(ps[:, b0 + hf, :], mx[i][:, kc, sub * 128:(sub + 1) * 128],
                                   wo[:, kc, hf * 512:(hf + 1) * 512], start=(kc == 0), stop=(kc == 15),
                                   r=[("mx", i), "wo"], w=[("ps", b0)])
                        pin = ps[:, b0:b0 + 2, :].rearrange("p a b -> p (a b)")
                        act(junk[:], pin, AF.Square, r=[("ps", b0), "st6"], w=[("st6a", t)],
                            accum_out=st6[:, t, 0:1])
                        rstd_ops(st6[:, t, 0:1], st6[:, t, 2:3], st6[:, t, 1:2], 1.0 / 1024,
                                 ("st6a", t), ("st6c", t), ("st6b", t))
                        stt("dve", yt[xi][:], pin, st6[:, t, 2:3], npost[:], ALU.mult, ALU.mult,
                            r=[("ps", b0), ("st6c", t), "npost"], w=[("yt", xi)])
                        tt("pool", yt[xi][:], yt[xi][:], xt[xi][:], ALU.add, r=[("yt", xi), ("xt6", xi)], w=[("yt", xi)])
                        dma("pool", out[t * 128:(t + 1) * 128, :], yt[xi][:], r=[("yt", xi)], w=[])
                S.barrier()
        S.barrier()
    return nc


def deltanet_phase(nc, S, E, T, H, ps, psb, cst, identb, cb, gates, bT_s, mixT_s, alog_d, dtb_d, dw_d, SL):
    mm, tr, act, cp, tt, ts, stt, ms, dma, rstd_ops = (H[k] for k in
                                                       ("mm", "tr", "act", "cp", "tt", "ts", "stt", "ms", "dma", "rstd_ops"))
    NT = SL // 128
    ident_f = cst[:, 0, :]
    ones_f = cst[:, 1, :]
    B4 = [128, 4, 128]
    with contextlib.ExitStack() as p5:
        qkT = T(p5, "qkT", [128, 8, SL], BF16)
        ofwd = T(p5, "ofwd", [128, NT, 4, 128], BF16)
        dwt = T(p5, "dwt", [128, 1, 128], F32)
        alg = T(p5, "alg", [128, 1, 8], F32)
        dtbt = T(p5, "dtbt", [128, 1, 8], F32)
        nA = T(p5, "nA", [128, 1, 8], F32)
        G = T(p5, "G", [128, NT, 8], F32)
        LB = T(p5, "LB", [128, NT, 8], F32)
        tmpg = T(p5, "tmpg", [128, NT, 8], F32)
        GC = T(p5, "GC", [128, NT, 8, 1], F32)
        TOT = T(p5, "TOT", [128, NT, 8], F32)
        GCB = T(p5, "GCB", [128, NT, 8, 1], F32)
        EGBP = T(p5, "EGBP", [128, NT, 8, 1], F32)
        BETA = T(p5, "BETA", [128, NT, 8, 1], F32)
        ETAIL = T(p5, "ETAIL", [128, NT, 8, 1], F32)
        EDC = T(p5, "EDC", [128, NT, 8], F32)
        dbl = lambda name, shape, dt: [T(p5, "%s%d" % (name, i), shape, dt) for i in range(2)]
        vTt = dbl("vTt", B4, BF16)
        ktok = dbl("ktok", B4, BF16)
        vtok = dbl("vtok", B4, BF16)
        DG1 = T(p5, "DG", [128, 2, 4, 128], F32)
        DG = [DG1, DG1]
        EE = dbl("EE", [128, 2, 4, 128], F32)
        D2 = dbl("D2", [128, 2, 4, 128], F32)
        EGB1 = T(p5, "EGB", B4, F32)
        EGB = [EGB1, EGB1]
        qh = dbl("qh", B4, BF16)
        tmpA = dbl("tmpA", B4, F32)
        tmpB = dbl("tmpB", B4, F32)
        intra = dbl("intra", B4, BF16)
        Ak = [dbl("Ak%d" % p, B4, F32) for p in range(2)]
        Mk = [dbl("Mk%d" % p, B4, F32) for p in range(2)]
        Xk = [dbl("Xk%d" % p, B4, F32) for p in range(2)]
        Xb = dbl("Xb", B4, BF16)
        vb = dbl("vb", B4, BF16)
        ke = dbl("ke", B4, BF16)
        ktl = dbl("ktl", B4, BF16)
        uu = dbl("uu", B4, F32)
        wT = dbl("wT", B4, BF16)
        vnew = dbl("vnew", B4, BF16)
        Sts = [T(p5, "St%d" % i, B4, F32) for i in range(2)]
        Sbs = [T(p5, "Sb%d" % i, B4, BF16) for i in range(2)]
        osum = dbl("osum", B4, F32)
        sso = dbl("sso", [128, 3, 4, 1], F32)
        yb = dbl("yb", B4, BF16)
        yT = dbl("yT", B4, BF16)

        dma("sp", qkT[:], bT_s[0:8].rearrange("c p t -> p c t"), w=["qkT"])
        dma("pool", dwt[:, 0, :], dw_d, w=["dwt"])
        dma("pool", alg[:, 0, :], alog_d, w=["alg"])
        dma("pool", dtbt[:, 0, :], dtb_d, w=["dtbt"])
        for i in range(2):
            ms("dve", sso[i][:], 0.0, w=[("sso", i)])
        act(nA[:], alg[:], AF.Exp, r=["alg"], w=["nA"])
        ts("dve", nA[:], nA[:], -1.0, None, ALU.mult, r=["nA"], w=["nA"])
        tt("dve", tmpg[:], gates[:, :, 0:8], dtbt[:].to_broadcast([128, NT, 8]), ALU.add, r=["gates", "dtbt"], w=["tmpg"])
        act(tmpg[:], tmpg[:], AF.Exp, r=["tmpg"], w=["tmpg"])
        act(tmpg[:], tmpg[:], AF.Ln, r=["tmpg", "cb"], w=["tmpg"], bias=cb[:, 1:2])
        tt("dve", G[:], tmpg[:], nA[:].to_broadcast([128, NT, 8]), ALU.mult, r=["tmpg", "nA"], w=["G"])
        act(LB[:], gates[:, :, 8:16], AF.Exp, r=["gates"], w=["LB"], scale=-1.0)
        act(LB[:], LB[:], AF.Ln, r=["LB", "cb"], w=["LB"], bias=cb[:, 1:2])
        ts("dve", LB[:], LB[:], -1.0, None, ALU.mult, r=["LB"], w=["LB"])
        mm(ps[:, 0, 0:NT * 4].rearrange("p (t c) -> p t c", c=4), cst[:, 2, :], G[:, :, 0:4], r=["cst", "G"], w=[("ps", 0)])
        mm(ps[:, 1, 0:NT * 4].rearrange("p (t c) -> p t c", c=4), cst[:, 3, :], G[:, :, 4:8], r=["cst", "G"], w=[("ps", 1)])
        mm(ps[:, 2, 0:NT * 8].rearrange("p (t c) -> p t c", c=8), ones_f, G[:, :, :], r=["cst", "G"], w=[("ps", 2)])
        cp("dve", GC[:, :, 0:4, 0], ps[:, 0, 0:NT * 4].rearrange("p (t c) -> p t c", c=4), r=[("ps", 0)], w=["GC"])
        cp("dve", GC[:, :, 4:8, 0], ps[:, 1, 0:NT * 4].rearrange("p (t c) -> p t c", c=4), r=[("ps", 1)], w=["GC"])
        cp("dve", TOT[:], ps[:, 2, 0:NT * 8].rearrange("p (t c) -> p t c", c=8), r=[("ps", 2)], w=["TOT"])
        tt("dve", GCB[:, :, :, 0], GC[:, :, :, 0], LB[:], ALU.add, r=["GC", "LB"], w=["GCB"])
        act(EGBP[:, :, :, 0], GCB[:, :, :, 0], AF.Exp, r=["GCB"], w=["EGBP"])
        act(BETA[:, :, :, 0], LB[:], AF.Exp, r=["LB"], w=["BETA"])
        tt("dve", tmpg[:], TOT[:], GC[:, :, :, 0], ALU.subtract, r=["TOT", "GC", "tmpg"], w=["tmpg2"])
        act(ETAIL[:, :, :, 0], tmpg[:], AF.Exp, r=["tmpg2"], w=["ETAIL"])
        act(EDC[:], TOT[:], AF.Exp, r=["TOT"], w=["EDC"])

        cnt = {0: 0, 1: 0}

        def nbank_d(d):
            b = d * 2 + cnt[d] % 2
            cnt[d] += 1
            return b

        hs = lambda h: slice(h * 128, (h + 1) * 128)
        v4 = lambda a: a.rearrange("p (h d) -> p h d", h=4)
        def chunk_gen(d, t, first):
            c0 = d * 4
            mask_s = cst[:, 4 + d:5 + d, :]
            mask_i = cst[:, 2 + d:3 + d, :]
            p = d
            St, Sb = Sts[d], Sbs[d]
            tk = slice(t * 128, (t + 1) * 128)
            K = lambda name: (name, p)
            yield
            dma("sp", vTt[p][:], bT_s[8:12].rearrange("c p t -> p c t")[:, :, tk], w=[K("vTt")])
            b = nbank_d(d)
            for h in range(4):
                tr(psb(b)[:, hs(h)], qkT[:, 4 + h, tk], r=["qkT"], w=[("ps", b)])
            cp("act", ktok[p][:], v4(psb(b)[:, 0:512]), r=[("ps", b)], w=[K("ktok")])
            b = nbank_d(d)
            for h in range(4):
                tr(psb(b)[:, hs(h)], vTt[p][:, h, :], r=[K("vTt")], w=[("ps", b)])
            cp("act", vtok[p][:], v4(psb(b)[:, 0:512]), r=[("ps", b)], w=[K("vtok")])
            yield
            bkk = nbank_d(d)
            for h in range(4):
                mm(ps[:, bkk, hs(h)], qkT[:, 4 + h, tk], qkT[:, 4 + h, tk], r=["qkT"], w=[("ps", bkk)])
            bqk = nbank_d(d)
            for h in range(4):
                mm(ps[:, bqk, hs(h)], qkT[:, 4 + h, tk], qkT[:, h, tk], r=["qkT"], w=[("ps", bqk)])
            stt("dve", tmpA[p][:], v4(ps[:, bkk, :]), -1.0, mask_s.to_broadcast(B4), ALU.mult, ALU.mult,
                r=[("ps", bkk), "cst"], w=[K("tmpA")])
            stt("dve", tmpB[p][:], v4(ps[:, bqk, :]), SCALE, mask_i.to_broadcast(B4), ALU.mult, ALU.mult,
                r=[("ps", bqk), "cst"], w=[K("tmpB")])
            yield
            tt("pool", DG[p][:, 0], ident_f.unsqueeze(1).to_broadcast(B4), GC[:, t, c0:c0 + 4, :].to_broadcast(B4), ALU.mult,
               r=["cst", "GC"], w=["DG0"])
            tt("pool", DG[p][:, 1], ident_f.unsqueeze(1).to_broadcast(B4), GCB[:, t, c0:c0 + 4, :].to_broadcast(B4), ALU.mult,
               r=["cst", "GCB"], w=["DG1"])
            bg = [nbank_d(d), nbank_d(d)]
            for k in range(2):
                for h in range(4):
                    mm(ps[:, bg[k], hs(h)], ones_f, DG[p][:, k, h, :], r=["cst", "DG%d" % k], w=[("ps", bg[k])])
                tt("dve", EE[p][:, k], v4(ps[:, bg[k], :]), GC[:, t, c0:c0 + 4, :].to_broadcast(B4), ALU.subtract,
                   r=[("ps", bg[k]), "GC"], w=[K("EE%d" % k)])
            act(EGB[p][:], v4(ps[:, bg[0], :]), AF.Exp, r=[("ps", bg[0]), "cb"], w=["EGB"], bias=cb[:, 2:3])
            tt("dve", qh[p][:], qkT[:, 0:4, tk], EGB[p][:], ALU.mult, r=["qkT", "EGB"], w=[K("qh")])
            ts("pool", EE[p][:], EE[p][:], 0.0, None, ALU.min, r=[K("EE0"), K("EE1")], w=[K("EE")])
            act(D2[p][:], EE[p][:], AF.Exp, r=[K("EE")], w=[K("D2")])
            yield
            A0 = Ak[p][0]
            tt("dve", A0[:], tmpA[p][:], D2[p][:, 1], ALU.mult, r=[K("tmpA"), K("D2")], w=[K("A0")])
            tt("dve", intra[p][:], tmpB[p][:], D2[p][:, 0], ALU.mult, r=[K("tmpB"), K("D2")], w=[K("intra")])
            yield
            b = nbank_d(d)
            for h in range(4):
                mm(ps[:, b, hs(h)], A0[:, h, :], ident_f, r=[K("A0"), "cst"], w=[("ps", b)])
            cp("act", Mk[p][0][:], v4(ps[:, b, :]), r=[("ps", b)], w=[K("M0")])
            tt("dve", Xk[p][0][:], A0[:], ident_f.unsqueeze(1).to_broadcast(B4), ALU.add, r=[K("A0"), "cst"], w=[K("X0")])
            yield
            for k in range(1, 7):
                a_prev, m_prev, x_prev = Ak[p][(k - 1) % 2], Mk[p][(k - 1) % 2], Xk[p][(k - 1) % 2]
                a_cur, m_cur, x_cur = Ak[p][k % 2], Mk[p][k % 2], Xk[p][k % 2]
                kp, kc_ = (k - 1) % 2, k % 2
                if k < 6:
                    b = nbank_d(d)
                    for h in range(4):
                        mm(ps[:, b, hs(h)], m_prev[:, h, :], a_prev[:, h, :], r=[K("A%d" % kp), K("M%d" % kp)], w=[("ps", b)])
                    cp("act", a_cur[:], v4(ps[:, b, :]), r=[("ps", b)], w=[K("A%d" % kc_)])
                    yield
                    b = nbank_d(d)
                    for h in range(4):
                        S.op("pe", lambda b=b, h=h, a_cur=a_cur: E["pe"].transpose(ps[:, b, hs(h)], a_cur[:, h, :], ident_f),
                             [K("A%d" % kc_), "cst"], [("ps", b)])
                    cp("act", m_cur[:], v4(ps[:, b, :]), r=[("ps", b)], w=[K("M%d" % kc_)])
                    yield
                else:
                    b = nbank_d(d)
                    for h in range(4):
                        mm(ps[:, b, hs(h)], a_prev[:, h, :], m_prev[:, h, :], r=[K("A%d" % kp), K("M%d" % kp)], w=[("ps", b)])
                    cp("act", m_cur[:], v4(ps[:, b, :]), r=[("ps", b)], w=[K("M%d" % kc_)])
                    yield
                b = nbank_d(d)
                for h in range(4):
                    mm(ps[:, b, hs(h)], m_cur[:, h, :], x_prev[:, h, :], r=[K("M%d" % kc_), K("X%d" % kp)], w=[("ps", b)])
                if k < 6:
                    tt("dve", x_cur[:], x_prev[:], v4(ps[:, b, :]), ALU.add, r=[("ps", b), K("X%d" % kp)], w=[K("X%d" % kc_)])
                else:
                    tt("dve", Xb[p][:], x_prev[:], v4(ps[:, b, :]), ALU.add, r=[("ps", b), K("X%d" % kp)], w=[K("Xb")])
                yield
            tt("pool", vb[p][:], vtok[p][:], BETA[:, t, c0:c0 + 4, :].to_broadcast(B4), ALU.mult, r=[K("vtok"), "BETA"], w=[K("vb")])
            tt("pool", ke[p][:], ktok[p][:], EGBP[:, t, c0:c0 + 4, :].to_broadcast(B4), ALU.mult, r=[K("ktok"), "EGBP"], w=[K("ke")])
            tt("pool", ktl[p][:], ktok[p][:], ETAIL[:, t, c0:c0 + 4, :].to_broadcast(B4), ALU.mult, r=[K("ktok"), "ETAIL"], w=[K("ktl")])
            b = nbank_d(d)
            for h in range(4):
                mm(ps[:, b, hs(h)], Xb[p][:, h, :], vb[p][:, h, :], r=[K("Xb"), K("vb")], w=[("ps", b)])
            cp("act", uu[p][:], v4(ps[:, b, :]), r=[("ps", b)], w=[K("uu")])
            b = nbank_d(d)
            for h in range(4):
                mm(ps[:, b, hs(h)], ke[p][:, h, :], Xb[p][:, h, :], r=[K("Xb"), K("ke")], w=[("ps", b)])
            cp("act", wT[p][:], v4(ps[:, b, :]), r=[("ps", b)], w=[K("wT")])
            yield
            for h in range(4):
                bh = 4 + h
                kb = ("ps", bh)
                mm(ps[:, bh, 0:128], wT[p][:, h, :], Sb[:, h, :], r=[K("wT"), ("Sb", d, h)], w=[kb])
                tt("dve", vnew[p][:, h, :], uu[p][:, h, :], ps[:, bh, 0:128], ALU.subtract,
                   r=[K("uu"), kb], w=[("vnew", p, h)])
                mm(ps[:, bh, 128:256], qh[p][:, h, :], Sb[:, h, :], start=True, stop=False, r=[K("qh"), ("Sb", d, h)], w=[kb])
                mm(ps[:, bh, 128:256], intra[p][:, h, :], vnew[p][:, h, :], start=False, stop=True,
                   r=[K("intra"), ("vnew", p, h)], w=[kb])
                mm(ps[:, bh, 256:384], ktl[p][:, h, :], vnew[p][:, h, :], r=[K("ktl"), ("vnew", p, h)], w=[kb])
                stt("dve", St[:, h, :], St[:, h, :], EDC[:, t, c0 + h:c0 + h + 1], ps[:, bh, 256:384], ALU.mult, ALU.add,
                    r=[("St", d, h), kb, "EDC"], w=[("St", d, h)])
                cp("act", Sb[:, h, :], St[:, h, :], r=[("St", d, h)], w=[("Sb", d, h)])
                if first:
                    cp("act", ofwd[:, t, h, :], ps[:, bh, 128:256], r=[kb], w=[("ofwd", t)])
                else:
                    tt("dve", osum[p][:, h, :], ps[:, bh, 128:256], ofwd[:, t, h, :], ALU.add,
                       r=[kb, ("ofwd", t)], w=[("osum", p, h)])
            yield
            if not first:
                ro = [("osum", p, h) for h in range(4)]
                tt("pool", tmpA[p][:], osum[p][:], osum[p][:], ALU.mult, r=ro, w=[K("tmpA")])
                S.op("dve", lambda p=p: E["dve"].tensor_reduce(out=sso[p][:, 0, :, 0], in_=tmpA[p][:], axis=AX.X, op=ALU.add),
                     [K("tmpA")], [K("sso_a")])
                rstd_ops(sso[p][:, 0, :, 0], sso[p][:, 2, :, 0], sso[p][:, 1, :, 0], 1.0 / 128, K("sso_a"), K("sso_c"), K("sso_b"))
                tt("dve", osum[p][:], osum[p][:], sso[p][:, 2, :, :].to_broadcast(B4), ALU.mult,
                   r=ro + [K("sso_c")], w=[K("osn")])
                tt("dve", yb[p][:], osum[p][:], dwt[:].to_broadcast(B4), ALU.mult, r=[K("osn"), "dwt"], w=[K("yb")])
                b = nbank_d(d)
                for h in range(4):
                    tr(psb(b)[:, hs(h)], yb[p][:, h, :], r=[K("yb")], w=[("ps", b)])
                cp("act", yT[p][:], v4(psb(b)[:, 0:512]), r=[("ps", b)], w=[K("yT")])
                dma("act", mixT_s[8:12].rearrange("c p t -> p c t")[:, :, tk], yT[p][:], r=[K("yT")], w=[])

        for d in range(2):
            ms("dve", Sts[d][:], 0.0, w=[("St", d, h) for h in range(4)])
            ms("dve", Sbs[d][:], 0.0, w=[("Sb", d, h) for h in range(4)])
        def stream(d):
            for i in range(NT):
                yield from chunk_gen(d, i if d == 0 else NT - 1 - i, i < NT // 2)

        gens = [stream(0), stream(1)]
        for _ in range(DN_LAG):
            next(gens[0])
        while gens:
            for g_ in list(gens):
                try:
                    next(g_)
                except StopIteration:
                    gens.remove(g_)
        S.barrier()


def host_consts(SL):
    t = np.arange(SL)
    inv = 10000.0 ** (-np.arange(0, 64, 2, dtype=np.float32) / 64.0)
    ang_r = (t // 64).astype(np.float32)[:, None] * inv[None, :]
    ang_c = (t % 64).astype(np.float32)[:, None] * inv[None, :]
    cos = np.stack([np.cos(ang_r), np.cos(ang_c)], 1)
    sin = np.stack([np.sin(ang_r), np.sin(ang_c)], 1)
    cos8 = np.tile(cos, (1, 4, 1)).reshape(SL, 256)
    sin8 = np.tile(sin, (1, 4, 1)).reshape(SL, 256)
    rope = np.concatenate([cos8, sin8], 1).astype(np.float32)
    j = np.arange(128)[:, None]
    i = np.arange(128)[None, :]
    ident = (i == j).astype(np.float32)
    ones = np.ones((128, 128), np.float32)
    U = (i >= j).astype(np.float32)
    Lo = (i <= j).astype(np.float32)
    Us = (i > j).astype(np.float32)
    Ls = (i < j).astype(np.float32)
    cst = np.stack([ident, ones, U, Lo, Us, Ls], 1).reshape(128, 6 * 128)
    return rope, np.ascontiguousarray(cst)


def make_in_maps(inputs, SL):
    rope, cst = host_consts(SL)
    rep = lambda v, n=128: np.ascontiguousarray(np.broadcast_to(np.asarray(v, np.float32).reshape(1, -1), (n, np.asarray(v).size)))
    l = 0
    shared = {
        "w_in": np.ascontiguousarray(inputs["w_in"][l], dtype=np.float32),
        "w_mem": np.ascontiguousarray(inputs["w_mem_kv"][l], dtype=np.float32),
        "w_out": np.ascontiguousarray(inputs["w_out"][l], dtype=np.float32),
        "npre": np.ascontiguousarray(np.asarray(inputs["norm_pre_w"][l], np.float32).reshape(8, 128).T),
        "nmem": np.ascontiguousarray(np.asarray(inputs["mem_norm_w"][l], np.float32).reshape(8, 128).T),
        "qw": rep(inputs["q_norm_w"][l]),
        "kw": rep(inputs["k_norm_w"][l]),
        "dw": rep(inputs["delta_norm_w"][l]),
        "npost": rep(inputs["norm_post_w"][l]),
        "convw": np.ascontiguousarray(np.asarray(inputs["conv_w"][l], np.float32).reshape(5, 12, 128).transpose(2, 1, 0).reshape(128, 60)),
        "alog": rep(np.asarray(inputs["a_log"][l]).reshape(-1)),
        "dtb": rep(np.asarray(inputs["dt_bias"][l]).reshape(-1)),
        "rope": rope,
        "cst": cst,
    }
    nb = inputs["x"].shape[0]
    maps = []
    for b in range(nb):
        m = dict(shared)
        m["x"] = np.ascontiguousarray(inputs["x"][b], dtype=np.float32)
        m["mem"] = np.ascontiguousarray(inputs["mem"][b], dtype=np.float32)
        maps.append(m)
    return maps


def kernel(**inputs):
    SL = inputs["x"].shape[1]
    nb = inputs["x"].shape[0]
    nc = build_nc(SL)
    maps = make_in_maps(inputs, SL)
    res = run_bass_kernel_spmd(nc, maps, core_ids=list(range(nb)))
    return np.stack([np.asarray(r["out"], dtype=np.float32) for r in res.results], 0)
```

```python
import contextlib
import math
import numpy as np
import ml_dtypes
import concourse.bass as bass
import concourse.mybir as mybir
from concourse.bass_utils import run_bass_kernel_spmd

F32 = mybir.dt.float32
BF16 = mybir.dt.bfloat16
ALU = mybir.AluOpType
AF = mybir.ActivationFunctionType
AX = mybir.AxisListType

D_MODEL = 1024
IN_WIDTH = 5648
N_MEM = 256
EPS = 1e-6
SCALE = 128 ** -0.5
DN_LAG = 0


class Sched:
    NSLOT = 12
    QUEUES = ("sp", "pool", "act")

    def __init__(self, nc, stack):
        self.nc = nc
        self.eng = {"pe": nc.tensor, "act": nc.scalar, "dve": nc.vector,
                    "pool": nc.gpsimd, "sp": nc.sync}
        self.sem = {e: stack.enter_context(nc.semaphore("s_" + e)) for e in self.eng}
        self.sigcount = {e: 0 for e in self.eng}
        self.waited = {e: {f: 0 for f in self.eng} for e in self.eng}
        self.dsem = {q: [stack.enter_context(nc.semaphore("d_%s%d" % (q, i)))
                         for i in range(self.NSLOT)] for q in self.QUEUES}
        self.duses = {q: [0] * self.NSLOT for q in self.QUEUES}
        self.dnext = {q: 0 for q in self.QUEUES}
        self.dwaited = {e: {q: [0] * self.NSLOT for q in self.QUEUES} for e in self.eng}
        self.pending = []
        self.lastw = {}
        self.readers = {}
        self.psum_last = {}
        self.n_inst = 0

    def op(self, eng, fn, reads=(), writes=()):
        self.pending.append(["c", eng, fn, tuple(reads), tuple(writes)])

    def dma(self, queue, fn, reads=(), writes=()):
        self.pending.append(["d", queue, fn, tuple(reads), tuple(writes)])

    def _wait_token(self, e, tok):
        if tok is None:
            return
        E = self.eng[e]
        if tok[0] == "e":
            _, f, sig = tok
            if f == e and e == "pe":
                return
            if self.waited[e][f] >= sig:
                return
            E.wait_ge(self.sem[f], sig)
            self.waited[e][f] = sig
            self.n_inst += 1
        else:
            _, q, slot, val = tok
            if self.dwaited[e][q][slot] >= val:
                return
            E.wait_ge(self.dsem[q][slot], val)
            self.dwaited[e][q][slot] = val
            self.n_inst += 1

    def flush(self):
        ops = self.pending
        self.pending = []
        n = len(ops)
        lastw = dict()
        readers = dict()
        deps_idx = [set() for _ in range(n)]
        deps_tok = [[] for _ in range(n)]
        has_dependents = [False] * n
        pl_batch = {}
        for i, (kind, e, fn, reads, writes) in enumerate(ops):
            for r in reads:
                if r in lastw:
                    deps_idx[i].add(lastw[r])
                elif r in self.lastw:
                    deps_tok[i].append(self.lastw[r])
            for w in writes:
                if w in lastw:
                    deps_idx[i].add(lastw[w])
                    for rd in readers.get(w, ()):
                        deps_idx[i].add(rd)
                else:
                    if w in self.lastw:
                        deps_tok[i].append(self.lastw[w])
                    for rd in readers.get(w, ()):
                        deps_idx[i].add(rd)
                    for rt in self.readers.get(w, ()):
                        deps_tok[i].append(rt)
            if kind == "c":
                for key in tuple(reads) + tuple(writes):
                    if isinstance(key, tuple) and key[0] in ("ps", "acc"):
                        bnk = key[1]
                        cur = pl_batch.setdefault(bnk, {})
                        for f, j in cur.items():
                            if f != e:
                                deps_idx[i].add(j)
                        for f, tk_ in self.psum_last.get(bnk, {}).items():
                            if f != e and f not in cur:
                                deps_tok[i].append(tk_)
                        cur[e] = i
            deps_idx[i].discard(i)
            for r in reads:
                readers.setdefault(r, []).append(i)
            for w in writes:
                lastw[w] = i
                readers[w] = []
                self.readers.pop(w, None)
                self.lastw.pop(w, None)
        for i in range(n):
            best = {}
            keep = set()
            for d in deps_idx[i]:
                if ops[d][0] == "c":
                    ed = ops[d][1]
                    if ed not in best or d > best[ed]:
                        best[ed] = d
                else:
                    keep.add(d)
            deps_idx[i] = keep | set(best.values())
        for i in range(n):
            for d in deps_idx[i]:
                kd, ed = ops[d][0], ops[d][1]
                ki, ei = ops[i][0], ops[i][1]
                if kd == "c" and not (ki == "c" and ei == ed and ed == "pe"):
                    has_dependents[d] = True
        last_on = {}
        for i, o in enumerate(ops):
            if o[0] == "c":
                last_on[o[1]] = i
        for i in last_on.values():
            has_dependents[i] = True
        for r, i in lastw.items():
            has_dependents[i] = True
        for r, lst in readers.items():
            for i in lst:
                has_dependents[i] = True
        for bnk, dct in pl_batch.items():
            for f, i in dct.items():
                has_dependents[i] = True
        tokens = [None] * n
        for i, (kind, e, fn, reads, writes) in enumerate(ops):
            for t in deps_tok[i]:
                self._wait_token(e, t)
            for d in sorted(deps_idx[i]):
                self._wait_token(e, tokens[d])
            if kind == "c":
                ins = fn()
                self.n_inst += 1
                if has_dependents[i]:
                    ins.then_inc(self.sem[e], 1)
                    self.sigcount[e] += 1
                    tokens[i] = ("e", e, self.sigcount[e])
                else:
                    tokens[i] = ("e", e, self.sigcount[e] + 1)
            else:
                q = e
                slot = self.dnext[q]
                self.dnext[q] = (slot + 1) % self.NSLOT
                prev = 16 * self.duses[q][slot]
                if prev > 0 and self.dwaited[q][q][slot] < prev:
                    self.eng[q].wait_ge(self.dsem[q][slot], prev)
                    self.dwaited[q][q][slot] = prev
                    self.n_inst += 1
                ins = fn()
                self.n_inst += 1
                ins.then_inc(self.dsem[q][slot], 16)
                self.duses[q][slot] += 1
                tokens[i] = ("d", q, slot, 16 * self.duses[q][slot])
        for r, i in lastw.items():
            self.lastw[r] = tokens[i]
        for r, lst in readers.items():
            self.readers[r] = [tokens[i] for i in lst]
        for bnk, dct in pl_batch.items():
            pd = self.psum_last.setdefault(bnk, {})
            for f, i in dct.items():
                pd[f] = tokens[i]

    def barrier(self):
        self.flush()
        for e in self.eng:
            for f in self.eng:
                if self.sigcount[f] > self.waited[e][f]:
                    self.eng[e].wait_ge(self.sem[f], self.sigcount[f])
                    self.waited[e][f] = self.sigcount[f]
            for q in self.QUEUES:
                for s in range(self.NSLOT):
                    v = 16 * self.duses[q][s]
                    if v > self.dwaited[e][q][s]:
                        self.eng[e].wait_ge(self.dsem[q][s], v)
                        self.dwaited[e][q][s] = v
        self.lastw = {}
        self.readers = {}
        self.psum_last = {}


def build_nc(SL=4096, dbg=False, phases=(1, 2, 3, 4, 5, 6), grp_limit=None):
    NT = SL // 128
    NB = SL // 512
    nc = bass.Bass("TRN2", target_bir_lowering=False)

    def din(name, shape, dt=F32):
        return nc.dram_tensor(name, shape, dt, kind="ExternalInput").ap()

    x = din("x", [SL, 1024])
    mem = din("mem", [N_MEM, 1024])
    w_in = din("w_in", [1024, IN_WIDTH])
    w_mem = din("w_mem", [1024, 1024])
    w_out = din("w_out", [2048, 1024])
    npre_d = din("npre", [128, 8])
    nmem_d = din("nmem", [128, 8])
    qw_d = din("qw", [128, 128])
    kw_d = din("kw", [128, 128])
    dw_d = din("dw", [128, 128])
    npost_d = din("npost", [128, 1024])
    convw_d = din("convw", [128, 12 * 5])
    alog_d = din("alog", [128, 8])
    dtb_d = din("dtb", [128, 8])
    rope_d = din("rope", [SL, 512])
    cst_d = din("cst", [128, 6 * 128])
    out = nc.dram_tensor("out", [SL, 1024], F32, kind="ExternalOutput").ap()
    sk = "ExternalOutput" if dbg else "Internal"

    def dscr(name, shape, dt=BF16):
        return nc.dram_tensor(name, shape, dt, kind=sk).ap()

    qT_s = dscr("qT_s", [8, 128, SL])
    kT_s = dscr("kT_s", [2, 128, SL])
    v_s = dscr("v_s", [SL, 256])
    bT_s = dscr("bT_s", [12, 128, SL])
    mqT_s = dscr("mqT_s", [4, 128, SL])
    szT_s = dscr("szT_s", [16, 128, SL])
    mixT_s = dscr("mixT_s", [16, 128, SL])
    gates_s = dscr("gates_s", [128, NT * 16], F32)

    with contextlib.ExitStack() as top:
        S = Sched(nc, top)
        E = S.eng
        ps = top.enter_context(nc.psum_tensor("ps", [128, 8, 512], F32))

        def psb(b):
            return ps[:, b, :].bitcast(BF16)

        def T(st, name, shape, dt):
            return st.enter_context(nc.sbuf_tensor("sb_" + name, shape, dt))

        cst = T(top, "cst", [128, 6, 128], F32)
        identb = T(top, "identb", [128, 128], BF16)
        onesb = T(top, "onesb", [128, 128], BF16)
        cb = T(top, "cb", [128, 4], F32)
        gates = T(top, "gates", [128, NT, 16], F32)
        S.dma("sp", lambda: E["sp"].dma_start(out=cst[:], in_=cst_d.rearrange("p (a b) -> p a b", a=6)), writes=["cst"])
        S.op("dve", lambda: E["dve"].tensor_copy(identb[:], cst[:, 0, :]), reads=["cst"], writes=["identb"])
        S.op("dve", lambda: E["dve"].tensor_copy(onesb[:], cst[:, 1, :]), reads=["cst"], writes=["onesb"])
        S.op("dve", lambda: E["dve"].memset(cb[:, 0:1], EPS), writes=["cb"])
        S.op("dve", lambda: E["dve"].memset(cb[:, 1:2], 1.0), writes=["cb"])
        S.op("dve", lambda: E["dve"].memset(cb[:, 2:3], math.log(SCALE)), writes=["cb"])
        S.op("dve", lambda: E["dve"].memset(cb[:, 3:4], 0.0), writes=["cb"])
        S.barrier()

        def mm(out_, lhsT, rhs, start=True, stop=True, r=(), w=(), skip=False):
            if skip:
                S.op("pe", lambda: E["pe"].matmul(out_, lhsT, rhs, start=start, stop=stop, skip_group_check=True), r, w)
            else:
                S.op("pe", lambda: E["pe"].matmul(out_, lhsT, rhs, start=start, stop=stop), r, w)

        def tr(out_, in_, r=(), w=()):
            S.op("pe", lambda: E["pe"].transpose(out_, in_, identb[:]), tuple(r) + ("identb",), w)

        def act(out_, in_, func, r=(), w=(), **kw):
            S.op("act", lambda: E["act"].activation(out=out_, in_=in_, func=func, **kw), r, w)

        def cp(eng, out_, in_, r=(), w=()):
            if eng == "act":
                S.op("act", lambda: E["act"].copy(out_, in_), r, w)
            else:
                S.op(eng, lambda: E[eng].tensor_copy(out_, in_), r, w)

        def tt(eng, out_, a, b, op, r=(), w=()):
            S.op(eng, lambda: E[eng].tensor_tensor(out_, a, b, op), r, w)

        def ts(eng, out_, a, s1, s2, op0, op1=None, r=(), w=()):
            if op1 is None:
                S.op(eng, lambda: E[eng].tensor_scalar(out_, a, s1, s2, op0), r, w)
            else:
                S.op(eng, lambda: E[eng].tensor_scalar(out_, a, s1, s2, op0, op1), r, w)

        def stt(eng, out_, a, sc, b, op0, op1, r=(), w=()):
            S.op(eng, lambda: E[eng].scalar_tensor_tensor(out=out_, in0=a, scalar=sc, in1=b, op0=op0, op1=op1), r, w)

        def ms(eng, out_, val, w=()):
            S.op(eng, lambda: E[eng].memset(out_, val), (), w)

        def dma(q, out_, in_, r=(), w=()):
            S.dma(q, lambda: E[q].dma_start(out=out_, in_=in_), r, w)

        def rstd_ops(src, dst, tmp, scale, key_src, key_dst, key_tmp):
            act(tmp, src, AF.Ln, r=[key_src, "cb"], w=[key_tmp], scale=scale, bias=cb[:, 0:1])
            act(dst, tmp, AF.Exp, r=[key_tmp], w=[key_dst], scale=-0.5)

        def run_pipelined(gens, depth):
            pending = list(gens)
            active = []
            while pending or active:
                if pending and len(active) < depth:
                    active.append(pending.pop(0))
                for g_ in list(active):
                    try:
                        next(g_)
                    except StopIteration:
                        active.remove(g_)

        H = dict(mm=mm, tr=tr, act=act, cp=cp, tt=tt, ts=ts, stt=stt, ms=ms, dma=dma, rstd_ops=rstd_ops)

        if 1 in phases:
            with contextlib.ExitStack() as p12:
                hT = T(p12, "hT", [128, 8, SL], BF16)
                with contextlib.ExitStack() as p1:
                    xt = [T(p1, "xt%d" % i, [128, 1024], F32) for i in range(2)]
                    hb = [T(p1, "hb%d" % i, [128, 1024], BF16) for i in range(2)]
                    junk = T(p1, "junk", [128, 1024], BF16)
                    st1 = T(p1, "st1", [128, NT, 4], F32)
                    ms("dve", st1[:], 0.0, w=["st1"])
                    for t in range(NT):
                        i = t % 2
                        dma("sp", xt[i][:], x[t * 128:(t + 1) * 128, :], w=[("xt", i)])
                        act(junk[:], xt[i][:], AF.Square, r=[("xt", i), "st1"], w=[("st1a", t)],
                            accum_out=st1[:, t, 0:1])
                        rstd_ops(st1[:, t, 0:1], st1[:, t, 2:3], st1[:, t, 1:2], 1.0 / 1024,
                                 ("st1a", t), ("st1c", t), ("st1b", t))
                        ts("dve", hb[i][:], xt[i][:], st1[:, t, 2:3], None, ALU.mult,
                           r=[("xt", i), ("st1c", t)], w=[("hb", i)])
                        bk = 4 + i
                        for kc in range(8):
                            tr(psb(bk)[:, kc * 128:(kc + 1) * 128], hb[i][:, kc * 128:(kc + 1) * 128],
                               r=[("hb", i)], w=[("ps", bk)])
                        cp("act" if t % 2 == 0 else "dve", hT[:, :, t * 128:(t + 1) * 128],
                           psb(bk).rearrange("p (k t) -> p k t", k=8), r=[("ps", bk)], w=["hT"])
                    S.barrier()

                with contextlib.ExitStack() as p2:
                  if 2 in phases:
                      wst1 = T(p2, "wst", [128, 8, 512], F32)
                      wst = [wst1, wst1]
                      wbf = [T(p2, "wbf%d" % i, [128, 8, 512], BF16) for i in range(2)]
                      npre = T(p2, "npre", [128, 8, 1], F32)
                      qwt = T(p2, "qwt", [128, 1, 128], F32)
                      kwt = T(p2, "kwt", [128, 1, 128], F32)
                      cwt = T(p2, "cwt", [128, 12, 5], F32)
                      ropet = [T(p2, "ropet%d" % i, [128, 512], F32) for i in range(4)]
                      sq = [T(p2, "sq%d" % i, [128, 512], F32) for i in range(4)]
                      xn = [T(p2, "xn%d" % i, [128, 512], F32) for i in range(4)]
                      t2 = [T(p2, "t2%d" % i, [128, 512], F32) for i in range(4)]
                      qr = [T(p2, "qr%d" % i, [128, 512], BF16) for i in range(4)]
                      ss4 = [T(p2, "ss4%d" % i, [128, 3, 4, 1], F32) for i in range(4)]
                      qst = [T(p2, "qst%d" % i, [128, 4, 512], BF16) for i in range(2)]
                      vst = [T(p2, "vst%d" % i, [128, 256], BF16) for i in range(2)]
                      fst = [T(p2, "fst%d" % i, [128, 512], BF16) for i in range(4)]
                      preb = [T(p2, "preb%d" % i, [128, SL + 4], BF16) for i in range(2)]
                      dg = [T(p2, "dg%d" % i, [128, 5, 128], BF16) for i in range(2)]
                      sl = [T(p2, "sl%d" % i, [128, 512], F32) for i in range(4)]
                      sqb = [T(p2, "sqb%d" % i, [128, 512], BF16) for i in range(4)]
                      lnv = [T(p2, "lnv%d" % i, [128, 512], F32) for i in range(4)]
                      rsd = [T(p2, "rsd%d" % i, [128, 512], F32) for i in range(4)]

                      dma("pool", npre[:, :, 0], npre_d, w=["npre"])
                      dma("pool", qwt[:, 0, :], qw_d, w=["qwt"])
                      dma("pool", kwt[:, 0, :], kw_d, w=["kwt"])
                      dma("pool", cwt[:], convw_d.rearrange("p (a b) -> p a b", a=12), w=["cwt"])
                      for i in range(2):
                          ms("dve", preb[i][:, 0:2], 0.0, w=["pre_pad"])
                          ms("dve", preb[i][:, SL + 2:SL + 4], 0.0, w=["pre_pad"])
                      for i in range(4):
                          ms("dve", ss4[i][:], 0.0, w=[("ss4a", i), ("ss4b", i), ("ss4c", i)])

                      w_in_v = w_in.rearrange("(kc p) n -> p kc n", p=128)
                      groups = [("A", 0, 512, 0), ("A", 512, 512, 4), ("KV", 1024, 512, 0), ("G", 3072, 16, 0)]
                      groups += [("B", 1536 + 512 * i, 512, 4 * i) for i in range(3)]
                      groups += [("M", 3088, 512, 0)]
                      groups += [("Z", 3600 + 512 * i, 512, 4 * i) for i in range(4)]
                      cnt = {"bank": 0, "pp": 0, "f": 0, "bb": 0}

                      def nbank():
                          b = cnt["bank"] % 4
                          cnt["bank"] += 1
                          return b

                      def load_w(gi):
                          kind, c0, n, _ = groups[gi]
                          wb = gi % 2
                          dma("pool", wst[wb][:, :, 0:n], w_in_v[:, :, c0:c0 + n], w=["wst"])
                          tt("pool", wbf[wb][:, :, 0:n], wst[wb][:, :, 0:n], npre[:].to_broadcast([128, 8, n]), ALU.mult,
                             r=["wst", "npre"], w=[("wbf", wb)])

                      def qk_post(t, b, nh, wt, wkey, dst, h0):
                          i = cnt["pp"] % 4
                          cnt["pp"] += 1
                          W = nh * 128
                          pin = ps[:, b, 0:W]
                          hv = lambda a: a.rearrange("p (h d) -> p h d", h=nh)
                          dma("sp", ropet[i][:], rope_d[t * 128:(t + 1) * 128, :], w=[("ropet", i)])
                          for hh in range(nh):
                              act(sq[i][:, hh * 128:(hh + 1) * 128], pin[:, hh * 128:(hh + 1) * 128], AF.Square,
                                  r=[("ps", b), ("ss4a", i)], w=[("sq", i), ("ss4a", i)], accum_out=ss4[i][:, 0, hh, :])
                          yield
                          rstd_ops(ss4[i][:, 0, 0:nh, 0], ss4[i][:, 2, 0:nh, 0], ss4[i][:, 1, 0:nh, 0], 1.0 / 128,
                                   ("ss4a", i), ("ss4c", i), ("ss4b", i))
                          yield
                          tt("dve", hv(xn[i][:, 0:W]), hv(pin), ss4[i][:, 2, 0:nh, :].to_broadcast([128, nh, 128]), ALU.mult,
                             r=[("ps", b), ("ss4c", i)], w=[("xn", i)])
                          tt("dve", hv(xn[i][:, 0:W]), hv(xn[i][:, 0:W]), wt[:].to_broadcast([128, nh, 128]), ALU.mult,
                             r=[("xn", i), wkey], w=[("xn", i)])
                          yield
                          HA = nh * 2
                          rv = lambda a: a.rearrange("p (a s m) -> p a s m", a=HA, s=2)
                          xv, t1v, t2v, ov = rv(xn[i][:, 0:W]), rv(sq[i][:, 0:W]), rv(t2[i][:, 0:W]), rv(qr[i][:, 0:W])
                          cosv = ropet[i][:, 0:256].rearrange("p (a o m) -> p a o m", a=8, o=1)[:, 0:HA]
                          sinv = ropet[i][:, 256:512].rearrange("p (a m) -> p a m", a=8)[:, 0:HA]
                          tt("dve", t1v, xv, cosv.to_broadcast([128, HA, 2, 32]), ALU.mult,
                             r=[("xn", i), ("ropet", i)], w=[("sq", i)])
                          tt("dve", t2v[:, :, 0, :], xv[:, :, 1, :], sinv, ALU.mult, r=[("xn", i), ("ropet", i)], w=[("t2", i)])
                          tt("dve", t2v[:, :, 1, :], xv[:, :, 0, :], sinv, ALU.mult, r=[("xn", i), ("ropet", i)], w=[("t2", i)])
                          yield
                          tt("dve", ov[:, :, 0, :], t1v[:, :, 0, :], t2v[:, :, 0, :], ALU.subtract,
                             r=[("sq", i), ("t2", i)], w=[("qr", i)])
                          tt("dve", ov[:, :, 1, :], t1v[:, :, 1, :], t2v[:, :, 1, :], ALU.add,
                             r=[("sq", i), ("t2", i)], w=[("qr", i)])
                          yield
                          bk = 4 + i % 2
                          for hh in range(nh):
                              tr(psb(bk)[:, hh * 128:(hh + 1) * 128], qr[i][:, hh * 128:(hh + 1) * 128],
                                 r=[("qr", i)], w=[("ps", bk)])
                          si = (t // 4) % 2
                          cp("act", qst[si][:, 0:nh, (t % 4) * 128:(t % 4 + 1) * 128],
                             psb(bk)[:, 0:W].rearrange("p (h t) -> p h t", h=nh), r=[("ps", bk)], w=[("qst", si)])
                          if t % 4 == 3:
                              tb = t // 4
                              dma("sp", dst[h0:h0 + nh].rearrange("h d t -> d h t")[:, :, tb * 512:(tb + 1) * 512],
                                  qst[si][:, 0:nh, :], r=[("qst", si)], w=[])

                      if grp_limit is not None:
                          groups[:] = [groups[k] for k in grp_limit]
                      load_w(0)
                      for gi, (kind, c0, n, i0) in enumerate(groups):
                          if gi + 1 < len(groups):
                              load_w(gi + 1)
                          wb = gi % 2
                          if kind in ("A", "KV", "G"):
                              def tile_task(t, kind=kind, n=n, wb=wb, i0=i0):
                                  b = nbank()
                                  for kc in range(8):
                                      mm(ps[:, b, 0:n], hT[:, kc, t * 128:(t + 1) * 128], wbf[wb][:, kc, 0:n],
                                         start=(kc == 0), stop=(kc == 7), r=["hT", ("wbf", wb)], w=[("ps", b)])
                                  yield
                                  if kind == "A":
                                      yield from qk_post(t, b, 4, qwt, "qwt", qT_s, i0)
                                  elif kind == "KV":
                                      vi = t % 2
                                      cp("act", vst[vi][:], ps[:, b, 256:512], r=[("ps", b)], w=[("vst", vi)])
                                      dma("sp", v_s[t * 128:(t + 1) * 128, :], vst[vi][:], r=[("vst", vi)], w=[])
                                      yield from qk_post(t, b, 2, kwt, "kwt", kT_s, 0)
                                  else:
                                      cp("act", gates[:, t, :], ps[:, b, 0:16], r=[("ps", b)], w=["gates"])
                              run_pipelined([tile_task(t) for t in range(NT)], 4)
                          elif kind in ("M", "Z"):
                              for cl in range(4):
                                  ct = i0 + cl
                                  for tb in range(NB):
                                      b = nbank()
                                      for kc in range(8):
                                          mm(ps[:, b, :], wbf[wb][:, kc, cl * 128:(cl + 1) * 128],
                                             hT[:, kc, tb * 512:(tb + 1) * 512], start=(kc == 0), stop=(kc == 7),
                                             r=["hT", ("wbf", wb)], w=[("ps", b)])
                                      fi = cnt["f"] % 2
                                      cnt["f"] += 1
                                      dst = mqT_s if kind == "M" else szT_s
                                      act(fst[fi][:], ps[:, b, :], AF.Copy if kind == "M" else AF.Silu,
                                          r=[("ps", b)], w=[("fst", fi)])
                                      dma("sp", dst[ct, :, tb * 512:(tb + 1) * 512], fst[fi][:], r=[("fst", fi)], w=[])
                          else:
                              pre_owner = {}
                              pcnt = {"b": 0, "c": 0}

                              def projA(cl, wb=wb, i0=i0):
                                  ct = i0 + cl
                                  pbi = ct % 2
                                  tt("dve", dg[pbi][:], identb[:].unsqueeze(1).to_broadcast([128, 5, 128]),
                                     cwt[:, ct, :].unsqueeze(2).to_broadcast([128, 5, 128]), ALU.mult,
                                     r=["identb", "cwt"], w=[("dg", pbi)])
                                  for tb in range(NB):
                                      b = pcnt["b"] % 2
                                      pcnt["b"] += 1
                                      for kc in range(8):
                                          mm(ps[:, b, :], wbf[wb][:, kc, cl * 128:(cl + 1) * 128],
                                             hT[:, kc, tb * 512:(tb + 1) * 512], start=(kc == 0), stop=(kc == 7),
                                             r=["hT", ("wbf", wb)], w=[("ps", b)])
                                      cp("dve", preb[pbi][:, 2 + tb * 512:2 + (tb + 1) * 512], ps[:, b, :],
                                         r=[("ps", b), "pre_pad"], w=[("pre", pbi, tb)])
                                      pre_owner[(pbi, tb)] = ct
                                      yield

                              def convB(cl, tb, i0=i0):
                                  ct = i0 + cl
                                  pbi = ct % 2
                                  o = tb * 512
                                  fi = pcnt["c"] % 4
                                  pcnt["c"] += 1
                                  bk = 2 + fi
                                  nbrs = list(range(max(0, tb - 1), min(NB, tb + 2)))
                                  for k in nbrs:
                                      assert pre_owner.get((pbi, k)) == ct, "conv block scheduled before its projection"
                                  prd = [("pre", pbi, k) for k in nbrs] + ["pre_pad", ("dg", pbi)]
                                  for j in range(5):
                                      mm(ps[:, bk, :], dg[pbi][:, j, :], preb[pbi][:, o + j:o + j + 512],
                                         start=(j == 0), stop=(j == 4), r=prd, w=[("ps", bk)])
                                  yield
                                  if ct >= 8:
                                      act(fst[fi][:], ps[:, bk, :], AF.Silu, r=[("ps", bk)], w=[("fst", fi)])
                                  else:
                                      act(lnv[fi][:], ps[:, bk, :], AF.Exp, r=[("ps", bk)], w=[("lnv", fi)], scale=-1.0)
                                      act(lnv[fi][:], lnv[fi][:], AF.Ln, r=[("lnv", fi), "cb"], w=[("lnv", fi)], bias=cb[:, 1:2])
                                      act(rsd[fi][:], lnv[fi][:], AF.Exp, r=[("lnv", fi)], w=[("rsd", fi)], scale=-1.0)
                                      yield
                                      tt("dve", sl[fi][:], ps[:, bk, :], rsd[fi][:], ALU.mult,
                                         r=[("ps", bk), ("rsd", fi)], w=[("sl", fi)])
                                      tt("pool", sqb[fi][:], sl[fi][:], sl[fi][:], ALU.mult, r=[("sl", fi)], w=[("sqb", fi)])
                                      yield
                                      b2 = 6 + fi % 2
                                      mm(ps[:, b2, :], onesb[:], sqb[fi][:], r=[("sqb", fi), "onesb"], w=[("ps", b2)])
                                      act(lnv[fi][:], ps[:, b2, :], AF.Ln, r=[("ps", b2), "cb"], w=[("lnv", fi)], bias=cb[:, 0:1])
                                      act(rsd[fi][:], lnv[fi][:], AF.Exp, r=[("lnv", fi)], w=[("rsd", fi)], scale=-0.5)
                                      yield
                                      tt("dve", fst[fi][:], sl[fi][:], rsd[fi][:], ALU.mult,
                                         r=[("sl", fi), ("rsd", fi)], w=[("fst", fi)])
                                  dma("sp", bT_s[ct, :, o:o + 512], fst[fi][:], r=[("fst", fi)], w=[])

                              tasks = [projA(0), projA(1)] + [convB(0, tb) for tb in range(NB)] + [projA(2)]
                              tasks += [convB(1, tb) for tb in range(NB)] + [projA(3)]
                              tasks += [convB(2, tb) for tb in range(NB)] + [convB(3, tb) for tb in range(NB)]
                              run_pipelined(tasks, 4)
                      if dbg:
                          dma("sp", gates_s, gates[:].rearrange("p t c -> p (t c)"), r=["gates"], w=["gates_s"])
                      S.barrier()

        def attn_finish(pa, ob, sbk, yst, dst_rows, qb, tag):
            S.op("dve", lambda: E["dve"].reciprocal(pa["rec"][:], ps[:, sbk, :]), [("ps", sbk)], ["rec"])
            tt("dve", yst[:], ps[:, ob, :], pa["rec"][:], ALU.mult, r=[("ps", ob), "rec"], w=[tag])
            dma("pool", dst_rows[:, qb * 512:(qb + 1) * 512], yst[:], r=[tag], w=[])

        if 3 in phases:
            with contextlib.ExitStack() as p3:
                kT = T(p3, "kT", [128, 2, SL], BF16)
                vp = T(p3, "vp", [128, NT, 2, 130], BF16)
                qt = [T(p3, "qt%d" % i, [128, 2, 512], BF16) for i in range(2)]
                pT = [T(p3, "pT%d" % i, [128, 1024], BF16) for i in range(3)]
                asum = T(p3, "asum", [128, 512], F32)
                oc = [T(p3, "oc%d" % i, [128, 512], F32) for i in range(2)]
                rec2 = [T(p3, "rec2%d" % i, [128, 512], F32) for i in range(2)]
                pa = {"rec": T(p3, "rec", [128, 512], F32)}
                yst = [T(p3, "yst%d" % i, [128, 512], BF16) for i in range(2)]
                dma("sp", kT[:], kT_s.rearrange("g d t -> d g t"), w=["kT"])
                ms("dve", vp[:], 1.0, w=["vp"])
                for g in range(2):
                    dma("pool", vp[:, :, g, 0:128], v_s[:, g * 128:(g + 1) * 128].rearrange("(t p) d -> p t d", p=128),
                        r=["vp"], w=["vp"])
                step = 0
                blk = 0
                for g in range(2):
                    for qb in range(NB):
                        for hp in range(2):
                            h0 = g * 4 + hp * 2
                            qi = blk % 2
                            blk += 1
                            dma("sp", qt[qi][:], qT_s[h0:h0 + 2].rearrange("h d t -> d h t")[:, :, qb * 512:(qb + 1) * 512],
                                w=[("qt", qi)])
                            def issue_s(kt, pb):
                                for j in range(2):
                                    mm(ps[:, pb + j, :], kT[:, g, kt * 128:(kt + 1) * 128], qt[qi][:, j, :],
                                       r=["kT", ("qt", qi)], w=[("ps", pb)])

                            issue_s(0, 2 * (step % 2))
                            for kt in range(NT):
                                pb = 2 * (step % 2)
                                pi = step % 3
                                step += 1
                                if kt + 1 < NT:
                                    issue_s(kt + 1, 2 * (step % 2))
                                act(pT[pi][:], ps[:, pb:pb + 2, :].rearrange("p a b -> p (a b)"), AF.Exp,
                                    r=[("ps", pb)], w=[("pT", pi)], scale=SCALE)
                                for j in range(2):
                                    mm(ps[:, 4 + j, :], vp[:, kt, g, 0:128], pT[pi][:, j * 512:(j + 1) * 512],
                                       start=(kt == 0), stop=(kt == NT - 1), r=[("pT", pi), "vp"], w=[("ps", 4 + j)])
                                mm(ps[:, 6, :], onesb[:], pT[pi][:, 0:512], start=(kt == 0), stop=(kt == NT - 1),
                                   r=[("pT", pi), "onesb"], w=[("ps", 6)])
                                if kt == 0:
                                    cp("dve", asum[:], pT[pi][:, 512:1024], r=[("pT", pi)], w=["asum"])
                                else:
                                    tt("dve", asum[:], asum[:], pT[pi][:, 512:1024], ALU.add, r=[("pT", pi), "asum"], w=["asum"])
                            for j in range(2):
                                cp("act", oc[j][:], ps[:, 4 + j, :], r=[("ps", 4 + j)], w=[("oc", j)])
                            mm(ps[:, 7, :], cst[:, 1, :], asum[:], r=["asum", "cst"], w=[("ps", 7)])
                            for j in range(2):
                                S.op("dve", lambda j=j: E["dve"].reciprocal(rec2[j][:], ps[:, 6 + j, :]), [("ps", 6 + j)], [("rec2", j)])
                                tt("dve", yst[j][:], oc[j][:], rec2[j][:], ALU.mult, r=[("oc", j), ("rec2", j)], w=[("yst", j)])
                                dma("pool", mixT_s[h0 + j][:, qb * 512:(qb + 1) * 512], yst[j][:], r=[("yst", j)], w=[])
                S.barrier()

        if 4 in phases:
            with contextlib.ExitStack() as p4:
                mt = [T(p4, "mt%d" % i, [128, 1024], F32) for i in range(2)]
                mb = [T(p4, "mb%d" % i, [128, 1024], BF16) for i in range(2)]
                junk = T(p4, "junk4", [128, 1024], BF16)
                st4 = T(p4, "st4", [128, 2, 4], F32)
                hmT = T(p4, "hmT", [128, 8, 256], BF16)
                wst = [T(p4, "wst4%d" % i, [128, 8, 512], F32) for i in range(2)]
                wmb = [T(p4, "wmb%d" % i, [128, 8, 512], BF16) for i in range(2)]
                nmem = T(p4, "nmem", [128, 8, 1], F32)
                mkT = T(p4, "mkT", [128, 4, 256], BF16)
                mvp = T(p4, "mvp", [128, 2, 4, 130], BF16)
                mq = [T(p4, "mq%d" % i, [128, 512], BF16) for i in range(2)]
                pT = [T(p4, "pT4%d" % i, [128, 512], BF16) for i in range(4)]
                pa = {"rec": T(p4, "rec4", [128, 512], F32)}
                yst = [T(p4, "yst4%d" % i, [128, 512], BF16) for i in range(2)]
                dma("pool", nmem[:, :, 0], nmem_d, w=["nmem"])
                ms("dve", st4[:], 0.0, w=["st4"])
                ms("dve", mvp[:], 1.0, w=["mvp"])
                w_mem_v = w_mem.rearrange("(kc p) n -> p kc n", p=128)
                for i in range(2):
                    dma("pool", wst[i][:], w_mem_v[:, :, i * 512:(i + 1) * 512], w=[("wst4", i)])
                    tt("pool", wmb[i][:], wst[i][:], nmem[:].to_broadcast([128, 8, 512]), ALU.mult,
                       r=[("wst4", i), "nmem"], w=[("wmb", i)])
                for t in range(2):
                    dma("sp", mt[t][:], mem[t * 128:(t + 1) * 128, :], w=[("mt", t)])
                    act(junk[:], mt[t][:], AF.Square, r=[("mt", t), "st4"], w=[("st4a", t)],
                        accum_out=st4[:, t, 0:1])
                    rstd_ops(st4[:, t, 0:1], st4[:, t, 2:3], st4[:, t, 1:2], 1.0 / 1024, ("st4a", t), ("st4c", t), ("st4b", t))
                    ts("dve", mb[t][:], mt[t][:], st4[:, t, 2:3], None, ALU.mult, r=[("mt", t), ("st4c", t)], w=[("mb", t)])
                    for kc in range(8):
                        tr(psb(t)[:, kc * 128:(kc + 1) * 128], mb[t][:, kc * 128:(kc + 1) * 128], r=[("mb", t)], w=[("ps", t)])
                    cp("act", hmT[:, :, t * 128:(t + 1) * 128], psb(t).rearrange("p (k t) -> p k t", k=8),
                       r=[("ps", t)], w=["hmT"])
                for hh in range(4):
                    b = 2 + hh % 2
                    for kc in range(8):
                        mm(ps[:, b, 0:256], wmb[0][:, kc, hh * 128:(hh + 1) * 128], hmT[:, kc, :],
                           start=(kc == 0), stop=(kc == 7), r=[("wmb", 0), "hmT"], w=[("ps", b)])
                    cp("act", mkT[:, hh, :], ps[:, b, 0:256], r=[("ps", b)], w=["mkT"])
                for kt in range(2):
                    b = 2 + kt
                    for kc in range(8):
                        mm(ps[:, b, :], hmT[:, kc, kt * 128:(kt + 1) * 128], wmb[1][:, kc, :],
                           start=(kc == 0), stop=(kc == 7), r=[("wmb", 1), "hmT"], w=[("ps", b)])
                    cp("act", mvp[:, kt, :, 0:128], ps[:, b, :].rearrange("p (h d) -> p h d", h=4),
                       r=[("ps", b), "mvp"], w=["mvp"])
                step = 0
                blk = 0
                for hh in range(4):
                    for qb in range(NB):
                        qi = blk % 2
                        blk += 1
                        dma("sp", mq[qi][:], mqT_s[hh, :, qb * 512:(qb + 1) * 512], w=[("mq", qi)])
                        ob = 4 + blk % 2
                        sbk = 6 + blk % 2
                        for kt in range(2):
                            sb = step % 4
                            pi = step % 4
                            step += 1
                            mm(ps[:, sb, :], mkT[:, hh, kt * 128:(kt + 1) * 128], mq[qi][:],
                               r=["mkT", ("mq", qi)], w=[("ps", sb)])
                            act(pT[pi][:], ps[:, sb, :], AF.Exp, r=[("ps", sb)], w=[("pT", pi)], scale=SCALE)
                            mm(ps[:, ob, :], mvp[:, kt, hh, 0:128], pT[pi][:], start=(kt == 0), stop=(kt == 1),
                               r=[("pT", pi), "mvp"], w=[("ps", ob)])
                            mm(ps[:, sbk, :], onesb[:], pT[pi][:], start=(kt == 0), stop=(kt == 1),
                               r=[("pT", pi), "onesb"], w=[("ps", sbk)])
                        yi = blk % 2
                        attn_finish(pa, ob, sbk, yst[yi], mixT_s[12 + hh], qb, ("yst", yi))
                S.barrier()

        if 5 in phases:
            deltanet_phase(nc, S, E, T, H, ps, psb, cst, identb, cb, gates, bT_s, mixT_s, alog_d, dtb_d, dw_d, SL)

        if 6 in phases:
            with contextlib.ExitStack() as p6:
                wo = T(p6, "wo", [128, 16, 1024], BF16)
                wst = [T(p6, "wst6%d" % i, [128, 4, 1024], F32) for i in range(2)]
                npost = T(p6, "npost", [128, 1024], F32)
                mx = [T(p6, "mx%d" % i, [128, 16, 512], BF16) for i in range(2)]
                sz = [T(p6, "sz%d" % i, [128, 16, 512], BF16) for i in range(2)]
                xt = [T(p6, "xt6%d" % i, [128, 1024], F32) for i in range(2)]
                yt = [T(p6, "yt%d" % i, [128, 1024], F32) for i in range(2)]
                junk = T(p6, "junk6", [128, 1024], BF16)
                st6 = T(p6, "st6", [128, NT, 4], F32)
                dma("pool", npost[:], npost_d, w=["npost"])
                ms("dve", st6[:], 0.0, w=["st6"])
                w_out_v = w_out.rearrange("(kc p) n -> p kc n", p=128)
                for c in range(4):
                    i = c % 2
                    dma("pool", wst[i][:], w_out_v[:, c * 4:(c + 1) * 4, :], w=[("wst6", i)])
                    cp("pool", wo[:, c * 4:(c + 1) * 4, :], wst[i][:], r=[("wst6", i)], w=["wo"])
                mixv = mixT_s.rearrange("c p t -> p c t")
                szv = szT_s.rearrange("c p t -> p c t")
                def load_gate(qb):
                    i = qb % 2
                    dma("sp", mx[i][:], mixv[:, :, qb * 512:(qb + 1) * 512], w=[("mx", i)])
                    dma("sp", sz[i][:], szv[:, :, qb * 512:(qb + 1) * 512], w=[("sz", i)])
                    for q4 in range(4):
                        tt("dve", mx[i][:, q4 * 4:(q4 + 1) * 4, :], mx[i][:, q4 * 4:(q4 + 1) * 4, :], sz[i][:, q4 * 4:(q4 + 1) * 4, :],
                           ALU.mult, r=[("mx", i), ("sz", i)], w=[("mx", i)])

                load_gate(0)
                for qb in range(NB):
                    i = qb % 2
                    if qb + 1 < NB:
                        load_gate(qb + 1)
                    for sub in range(4):
                        t = qb * 4 + sub
                        xi = t % 2
                        b0 = (t % 4) * 2
                        dma("sp", xt[xi][:], x[t * 128:(t + 1) * 128, :], w=[("xt6", xi)])
                        for hf in range(2):
                            for kc in range(16):
                                mm(ps[:, b0 + hf, :], mx[i][:, kc, sub * 128:(sub + 1) * 128],
                                   wo[:, kc, hf * 512:(hf + 1) * 512], start=(kc == 0), stop=(kc == 15),
                                   r=[("mx", i), "wo"], w=[("ps", b0)])
                        pin = ps[:, b0:b0 + 2, :].rearrange("p a b -> p (a b)")
                        act(junk[:], pin, AF.Square, r=[("ps", b0), "st6"], w=[("st6a", t)],
                            accum_out=st6[:, t, 0:1])
                        rstd_ops(st6[:, t, 0:1], st6[:, t, 2:3], st6[:, t, 1:2], 1.0 / 1024,
                                 ("st6a", t), ("st6c", t), ("st6b", t))
                        stt("dve", yt[xi][:], pin, st6[:, t, 2:3], npost[:], ALU.mult, ALU.mult,
                            r=[("ps", b0), ("st6c", t), "npost"], w=[("yt", xi)])
                        tt("pool", yt[xi][:], yt[xi][:], xt[xi][:], ALU.add, r=[("yt", xi), ("xt6", xi)], w=[("yt", xi)])
                        dma("pool", out[t * 128:(t + 1) * 128, :], yt[xi][:], r=[("yt", xi)], w=[])
                S.barrier()
        S.barrier()
    return nc


def deltanet_phase(nc, S, E, T, H, ps, psb, cst, identb, cb, gates, bT_s, mixT_s, alog_d, dtb_d, dw_d, SL):
    mm, tr, act, cp, tt, ts, stt, ms, dma, rstd_ops = (H[k] for k in
                                                       ("mm", "tr", "act", "cp", "tt", "ts", "stt", "ms", "dma", "rstd_ops"))
    NT = SL // 128
    ident_f = cst[:, 0, :]
    ones_f = cst[:, 1, :]
    B4 = [128, 4, 128]
    with contextlib.ExitStack() as p5:
        qkT = T(p5, "qkT", [128, 8, SL], BF16)
        ofwd = T(p5, "ofwd", [128, NT, 4, 128], BF16)
        dwt = T(p5, "dwt", [128, 1, 128], F32)
        alg = T(p5, "alg", [128, 1, 8], F32)
        dtbt = T(p5, "dtbt", [128, 1, 8], F32)
        nA = T(p5, "nA", [128, 1, 8], F32)
        G = T(p5, "G", [128, NT, 8], F32)
        LB = T(p5, "LB", [128, NT, 8], F32)
        tmpg = T(p5, "tmpg", [128, NT, 8], F32)
        GC = T(p5, "GC", [128, NT, 8, 1], F32)
        TOT = T(p5, "TOT", [128, NT, 8], F32)
        GCB = T(p5, "GCB", [128, NT, 8, 1], F32)
        EGBP = T(p5, "EGBP", [128, NT, 8, 1], F32)
        BETA = T(p5, "BETA", [128, NT, 8, 1], F32)
        ETAIL = T(p5, "ETAIL", [128, NT, 8, 1], F32)
        EDC = T(p5, "EDC", [128, NT, 8], F32)
        dbl = lambda name, shape, dt: [T(p5, "%s%d" % (name, i), shape, dt) for i in range(2)]
        vTt = dbl("vTt", B4, BF16)
        ktok = dbl("ktok", B4, BF16)
        vtok = dbl("vtok", B4, BF16)
        DG1 = T(p5, "DG", [128, 2, 4, 128], F32)
        DG = [DG1, DG1]
        EE = dbl("EE", [128, 2, 4, 128], F32)
        D2 = dbl("D2", [128, 2, 4, 128], F32)
        EGB1 = T(p5, "EGB", B4, F32)
        EGB = [EGB1, EGB1]
        qh = dbl("qh", B4, BF16)
        tmpA = dbl("tmpA", B4, F32)
        tmpB = dbl("tmpB", B4, F32)
        intra = dbl("intra", B4, BF16)
        Ak = [dbl("Ak%d" % p, B4, F32) for p in range(2)]
        Mk = [dbl("Mk%d" % p, B4, F32) for p in range(2)]
        Xk = [dbl("Xk%d" % p, B4, F32) for p in range(2)]
        Xb = dbl("Xb", B4, BF16)
        vb = dbl("vb", B4, BF16)
        ke = dbl("ke", B4, BF16)
        ktl = dbl("ktl", B4, BF16)
        uu = dbl("uu", B4, F32)
        wT = dbl("wT", B4, BF16)
        vnew = dbl("vnew", B4, BF16)
        Sts = [T(p5, "St%d" % i, B4, F32) for i in range(2)]
        Sbs = [T(p5, "Sb%d" % i, B4, BF16) for i in range(2)]
        osum = dbl("osum", B4, F32)
        sso = dbl("sso", [128, 3, 4, 1], F32)
        yb = dbl("yb", B4, BF16)
        yT = dbl("yT", B4, BF16)

        dma("sp", qkT[:], bT_s[0:8].rearrange("c p t -> p c t"), w=["qkT"])
        dma("pool", dwt[:, 0, :], dw_d, w=["dwt"])
        dma("pool", alg[:, 0, :], alog_d, w=["alg"])
        dma("pool", dtbt[:, 0, :], dtb_d, w=["dtbt"])
        for i in range(2):
            ms("dve", sso[i][:], 0.0, w=[("sso", i)])
        act(nA[:], alg[:], AF.Exp, r=["alg"], w=["nA"])
        ts("dve", nA[:], nA[:], -1.0, None, ALU.mult, r=["nA"], w=["nA"])
        tt("dve", tmpg[:], gates[:, :, 0:8], dtbt[:].to_broadcast([128, NT, 8]), ALU.add, r=["gates", "dtbt"], w=["tmpg"])
        act(tmpg[:], tmpg[:], AF.Exp, r=["tmpg"], w=["tmpg"])
        act(tmpg[:], tmpg[:], AF.Ln, r=["tmpg", "cb"], w=["tmpg"], bias=cb[:, 1:2])
        tt("dve", G[:], tmpg[:], nA[:].to_broadcast([128, NT, 8]), ALU.mult, r=["tmpg", "nA"], w=["G"])
        act(LB[:], gates[:, :, 8:16], AF.Exp, r=["gates"], w=["LB"], scale=-1.0)
        act(LB[:], LB[:], AF.Ln, r=["LB", "cb"], w=["LB"], bias=cb[:, 1:2])
        ts("dve", LB[:], LB[:], -1.0, None, ALU.mult, r=["LB"], w=["LB"])
        mm(ps[:, 0, 0:NT * 4].rearrange("p (t c) -> p t c", c=4), cst[:, 2, :], G[:, :, 0:4], r=["cst", "G"], w=[("ps", 0)])
        mm(ps[:, 1, 0:NT * 4].rearrange("p (t c) -> p t c", c=4), cst[:, 3, :], G[:, :, 4:8], r=["cst", "G"], w=[("ps", 1)])
        mm(ps[:, 2, 0:NT * 8].rearrange("p (t c) -> p t c", c=8), ones_f, G[:, :, :], r=["cst", "G"], w=[("ps", 2)])
        cp("dve", GC[:, :, 0:4, 0], ps[:, 0, 0:NT * 4].rearrange("p (t c) -> p t c", c=4), r=[("ps", 0)], w=["GC"])
        cp("dve", GC[:, :, 4:8, 0], ps[:, 1, 0:NT * 4].rearrange("p (t c) -> p t c", c=4), r=[("ps", 1)], w=["GC"])
        cp("dve", TOT[:], ps[:, 2, 0:NT * 8].rearrange("p (t c) -> p t c", c=8), r=[("ps", 2)], w=["TOT"])
        tt("dve", GCB[:, :, :, 0], GC[:, :, :, 0], LB[:], ALU.add, r=["GC", "LB"], w=["GCB"])
        act(EGBP[:, :, :, 0], GCB[:, :, :, 0], AF.Exp, r=["GCB"], w=["EGBP"])
        act(BETA[:, :, :, 0], LB[:], AF.Exp, r=["LB"], w=["BETA"])
        tt("dve", tmpg[:], TOT[:], GC[:, :, :, 0], ALU.subtract, r=["TOT", "GC", "tmpg"], w=["tmpg2"])
        act(ETAIL[:, :, :, 0], tmpg[:], AF.Exp, r=["tmpg2"], w=["ETAIL"])
        act(EDC[:], TOT[:], AF.Exp, r=["TOT"], w=["EDC"])

        cnt = {0: 0, 1: 0}

        def nbank_d(d):
            b = d * 2 + cnt[d] % 2
            cnt[d] += 1
            return b

        hs = lambda h: slice(h * 128, (h + 1) * 128)
        v4 = lambda a: a.rearrange("p (h d) -> p h d", h=4)
        def chunk_gen(d, t, first):
            c0 = d * 4
            mask_s = cst[:, 4 + d:5 + d, :]
            mask_i = cst[:, 2 + d:3 + d, :]
            p = d
            St, Sb = Sts[d], Sbs[d]
            tk = slice(t * 128, (t + 1) * 128)
            K = lambda name: (name, p)
            yield
            dma("sp", vTt[p][:], bT_s[8:12].rearrange("c p t -> p c t")[:, :, tk], w=[K("vTt")])
            b = nbank_d(d)
            for h in range(4):
                tr(psb(b)[:, hs(h)], qkT[:, 4 + h, tk], r=["qkT"], w=[("ps", b)])
            cp("act", ktok[p][:], v4(psb(b)[:, 0:512]), r=[("ps", b)], w=[K("ktok")])
            b = nbank_d(d)
            for h in range(4):
                tr(psb(b)[:, hs(h)], vTt[p][:, h, :], r=[K("vTt")], w=[("ps", b)])
            cp("act", vtok[p][:], v4(psb(b)[:, 0:512]), r=[("ps", b)], w=[K("vtok")])
            yield
            bkk = nbank_d(d)
            for h in range(4):
                mm(ps[:, bkk, hs(h)], qkT[:, 4 + h, tk], qkT[:, 4 + h, tk], r=["qkT"], w=[("ps", bkk)])
            bqk = nbank_d(d)
            for h in range(4):
                mm(ps[:, bqk, hs(h)], qkT[:, 4 + h, tk], qkT[:, h, tk], r=["qkT"], w=[("ps", bqk)])
            stt("dve", tmpA[p][:], v4(ps[:, bkk, :]), -1.0, mask_s.to_broadcast(B4), ALU.mult, ALU.mult,
                r=[("ps", bkk), "cst"], w=[K("tmpA")])
            stt("dve", tmpB[p][:], v4(ps[:, bqk, :]), SCALE, mask_i.to_broadcast(B4), ALU.mult, ALU.mult,
                r=[("ps", bqk), "cst"], w=[K("tmpB")])
            yield
            tt("dve", DG[p][:, 0], ident_f.unsqueeze(1).to_broadcast(B4), GC[:, t, c0:c0 + 4, :].to_broadcast(B4), ALU.mult,
               r=["cst", "GC"], w=["DG0"])
            tt("dve", DG[p][:, 1], ident_f.unsqueeze(1).to_broadcast(B4), GCB[:, t, c0:c0 + 4, :].to_broadcast(B4), ALU.mult,
               r=["cst", "GCB"], w=["DG1"])
            bg = [nbank_d(d), nbank_d(d)]
            for k in range(2):
                for h in range(4):
                    mm(ps[:, bg[k], hs(h)], ones_f, DG[p][:, k, h, :], r=["cst", "DG%d" % k], w=[("ps", bg[k])])
                tt("dve", EE[p][:, k], v4(ps[:, bg[k], :]), GC[:, t, c0:c0 + 4, :].to_broadcast(B4), ALU.subtract,
                   r=[("ps", bg[k]), "GC"], w=[K("EE%d" % k)])
            act(EGB[p][:], v4(ps[:, bg[0], :]), AF.Exp, r=[("ps", bg[0]), "cb"], w=["EGB"], bias=cb[:, 2:3])
            tt("dve", qh[p][:], qkT[:, 0:4, tk], EGB[p][:], ALU.mult, r=["qkT", "EGB"], w=[K("qh")])
            ts("dve", EE[p][:], EE[p][:], 0.0, None, ALU.min, r=[K("EE0"), K("EE1")], w=[K("EE")])
            act(D2[p][:], EE[p][:], AF.Exp, r=[K("EE")], w=[K("D2")])
            yield
            A0 = Ak[p][0]
            tt("dve", A0[:], tmpA[p][:], D2[p][:, 1], ALU.mult, r=[K("tmpA"), K("D2")], w=[K("A0")])
            tt("dve", intra[p][:], tmpB[p][:], D2[p][:, 0], ALU.mult, r=[K("tmpB"), K("D2")], w=[K("intra")])
            yield
            b = nbank_d(d)
            for h in range(4):
                mm(ps[:, b, hs(h)], A0[:, h, :], ident_f, r=[K("A0"), "cst"], w=[("ps", b)])
            cp("act", Mk[p][0][:], v4(ps[:, b, :]), r=[("ps", b)], w=[K("M0")])
            tt("dve", Xk[p][0][:], A0[:], ident_f.unsqueeze(1).to_broadcast(B4), ALU.add, r=[K("A0"), "cst"], w=[K("X0")])
            yield
            for k in range(1, 7):
                a_prev, m_prev, x_prev = Ak[p][(k - 1) % 2], Mk[p][(k - 1) % 2], Xk[p][(k - 1) % 2]
                a_cur, m_cur, x_cur = Ak[p][k % 2], Mk[p][k % 2], Xk[p][k % 2]
                kp, kc_ = (k - 1) % 2, k % 2
                if k < 6:
                    b = nbank_d(d)
                    for h in range(4):
                        mm(ps[:, b, hs(h)], m_prev[:, h, :], a_prev[:, h, :], r=[K("A%d" % kp), K("M%d" % kp)], w=[("ps", b)])
                    cp("act", a_cur[:], v4(ps[:, b, :]), r=[("ps", b)], w=[K("A%d" % kc_)])
                    yield
                    b = nbank_d(d)
                    for h in range(4):
                        S.op("pe", lambda b=b, h=h, a_cur=a_cur: E["pe"].transpose(ps[:, b, hs(h)], a_cur[:, h, :], ident_f),
                             [K("A%d" % kc_), "cst"], [("ps", b)])
                    cp("act", m_cur[:], v4(ps[:, b, :]), r=[("ps", b)], w=[K("M%d" % kc_)])
                    yield
                else:
                    b = nbank_d(d)
                    for h in range(4):
                        mm(ps[:, b, hs(h)], a_prev[:, h, :], m_prev[:, h, :], r=[K("A%d" % kp), K("M%d" % kp)], w=[("ps", b)])
                    cp("act", m_cur[:], v4(ps[:, b, :]), r=[("ps", b)], w=[K("M%d" % kc_)])
                    yield
                b = nbank_d(d)
                for h in range(4):
                    mm(ps[:, b, hs(h)], m_cur[:, h, :], x_prev[:, h, :], r=[K("M%d" % kc_), K("X%d" % kp)], w=[("ps", b)])
                if k < 6:
                    tt("dve", x_cur[:], x_prev[:], v4(ps[:, b, :]), ALU.add, r=[("ps", b), K("X%d" % kp)], w=[K("X%d" % kc_)])
                else:
                    tt("dve", Xb[p][:], x_prev[:], v4(ps[:, b, :]), ALU.add, r=[("ps", b), K("X%d" % kp)], w=[K("Xb")])
                yield
            tt("pool", vb[p][:], vtok[p][:], BETA[:, t, c0:c0 + 4, :].to_broadcast(B4), ALU.mult, r=[K("vtok"), "BETA"], w=[K("vb")])
            tt("pool", ke[p][:], ktok[p][:], EGBP[:, t, c0:c0 + 4, :].to_broadcast(B4), ALU.mult, r=[K("ktok"), "EGBP"], w=[K("ke")])
            tt("pool", ktl[p][:], ktok[p][:], ETAIL[:, t, c0:c0 + 4, :].to_broadcast(B4), ALU.mult, r=[K("ktok"), "ETAIL"], w=[K("ktl")])
            b = nbank_d(d)
            for h in range(4):
                mm(ps[:, b, hs(h)], Xb[p][:, h, :], vb[p][:, h, :], r=[K("Xb"), K("vb")], w=[("ps", b)])
            cp("act", uu[p][:], v4(ps[:, b, :]), r=[("ps", b)], w=[K("uu")])
            b = nbank_d(d)
            for h in range(4):
                mm(ps[:, b, hs(h)], ke[p][:, h, :], Xb[p][:, h, :], r=[K("Xb"), K("ke")], w=[("ps", b)])
            cp("act", wT[p][:], v4(ps[:, b, :]), r=[("ps", b)], w=[K("wT")])
            yield
            for h in range(4):
                bh = 4 + h
                kb = ("ps", bh)
                mm(ps[:, bh, 0:128], wT[p][:, h, :], Sb[:, h, :], r=[K("wT"), ("Sb", d, h)], w=[kb])
                tt("dve", vnew[p][:, h, :], uu[p][:, h, :], ps[:, bh, 0:128], ALU.subtract,
                   r=[K("uu"), kb], w=[("vnew", p, h)])
                mm(ps[:, bh, 128:256], qh[p][:, h, :], Sb[:, h, :], start=True, stop=False, r=[K("qh"), ("Sb", d, h)], w=[kb])
                mm(ps[:, bh, 128:256], intra[p][:, h, :], vnew[p][:, h, :], start=False, stop=True,
                   r=[K("intra"), ("vnew", p, h)], w=[kb])
                mm(ps[:, bh, 256:384], ktl[p][:, h, :], vnew[p][:, h, :], r=[K("ktl"), ("vnew", p, h)], w=[kb])
                stt("dve", St[:, h, :], St[:, h, :], EDC[:, t, c0 + h:c0 + h + 1], ps[:, bh, 256:384], ALU.mult, ALU.add,
                    r=[("St", d, h), kb, "EDC"], w=[("St", d, h)])
                cp("act", Sb[:, h, :], St[:, h, :], r=[("St", d, h)], w=[("Sb", d, h)])
                if first:
                    cp("act", ofwd[:, t, h, :], ps[:, bh, 128:256], r=[kb], w=[("ofwd", t)])
                else:
                    tt("dve", osum[p][:, h, :], ps[:, bh, 128:256], ofwd[:, t, h, :], ALU.add,
                       r=[kb, ("ofwd", t)], w=[("osum", p, h)])
            yield
            if not first:
                ro = [("osum", p, h) for h in range(4)]
                tt("pool", tmpA[p][:], osum[p][:], osum[p][:], ALU.mult, r=ro, w=[K("tmpA")])
                S.op("dve", lambda p=p: E["dve"].tensor_reduce(out=sso[p][:, 0, :, 0], in_=tmpA[p][:], axis=AX.X, op=ALU.add),
                     [K("tmpA")], [K("sso_a")])
                rstd_ops(sso[p][:, 0, :, 0], sso[p][:, 2, :, 0], sso[p][:, 1, :, 0], 1.0 / 128, K("sso_a"), K("sso_c"), K("sso_b"))
                tt("dve", osum[p][:], osum[p][:], sso[p][:, 2, :, :].to_broadcast(B4), ALU.mult,
                   r=ro + [K("sso_c")], w=[K("osn")])
                tt("dve", yb[p][:], osum[p][:], dwt[:].to_broadcast(B4), ALU.mult, r=[K("osn"), "dwt"], w=[K("yb")])
                b = nbank_d(d)
                for h in range(4):
                    tr(psb(b)[:, hs(h)], yb[p][:, h, :], r=[K("yb")], w=[("ps", b)])
                cp("act", yT[p][:], v4(psb(b)[:, 0:512]), r=[("ps", b)], w=[K("yT")])
                dma("act", mixT_s[8:12].rearrange("c p t -> p c t")[:, :, tk], yT[p][:], r=[K("yT")], w=[])

        for d in range(2):
            ms("dve", Sts[d][:], 0.0, w=[("St", d, h) for h in range(4)])
            ms("dve", Sbs[d][:], 0.0, w=[("Sb", d, h) for h in range(4)])
        def stream(d):
            for i in range(NT):
                yield from chunk_gen(d, i if d == 0 else NT - 1 - i, i < NT // 2)

        gens = [stream(0), stream(1)]
        for _ in range(DN_LAG):
            next(gens[0])
        while gens:
            for g_ in list(gens):
                try:
                    next(g_)
                except StopIteration:
                    gens.remove(g_)
        S.barrier()


def host_consts(SL):
    t = np.arange(SL)
    inv = 10000.0 ** (-np.arange(0, 64, 2, dtype=np.float32) / 64.0)
    ang_r = (t // 64).astype(np.float32)[:, None] * inv[None, :]
    ang_c = (t % 64).astype(np.float32)[:, None] * inv[None, :]
    cos = np.stack([np.cos(ang_r), np.cos(ang_c)], 1)
    sin = np.stack([np.sin(ang_r), np.sin(ang_c)], 1)
    cos8 = np.tile(cos, (1, 4, 1)).reshape(SL, 256)
    sin8 = np.tile(sin, (1, 4, 1)).reshape(SL, 256)
    rope = np.concatenate([cos8, sin8], 1).astype(np.float32)
    j = np.arange(128)[:, None]
    i = np.arange(128)[None, :]
    ident = (i == j).astype(np.float32)
    ones = np.ones((128, 128), np.float32)
    U = (i >= j).astype(np.float32)
    Lo = (i <= j).astype(np.float32)
    Us = (i > j).astype(np.float32)
    Ls = (i < j).astype(np.float32)
    cst = np.stack([ident, ones, U, Lo, Us, Ls], 1).reshape(128, 6 * 128)
    return rope, np.ascontiguousarray(cst)


def make_in_maps(inputs, SL):
    rope, cst = host_consts(SL)
    rep = lambda v, n=128: np.ascontiguousarray(np.broadcast_to(np.asarray(v, np.float32).reshape(1, -1), (n, np.asarray(v).size)))
    l = 0
    shared = {
        "w_in": np.ascontiguousarray(inputs["w_in"][l], dtype=np.float32),
        "w_mem": np.ascontiguousarray(inputs["w_mem_kv"][l], dtype=np.float32),
        "w_out": np.ascontiguousarray(inputs["w_out"][l], dtype=np.float32),
        "npre": np.ascontiguousarray(np.asarray(inputs["norm_pre_w"][l], np.float32).reshape(8, 128).T),
        "nmem": np.ascontiguousarray(np.asarray(inputs["mem_norm_w"][l], np.float32).reshape(8, 128).T),
        "qw": rep(inputs["q_norm_w"][l]),
        "kw": rep(inputs["k_norm_w"][l]),
        "dw": rep(inputs["delta_norm_w"][l]),
        "npost": rep(inputs["norm_post_w"][l]),
        "convw": np.ascontiguousarray(np.asarray(inputs["conv_w"][l], np.float32).reshape(5, 12, 128).transpose(2, 1, 0).reshape(128, 60)),
        "alog": rep(np.asarray(inputs["a_log"][l]).reshape(-1)),
        "dtb": rep(np.asarray(inputs["dt_bias"][l]).reshape(-1)),
        "rope": rope,
        "cst": cst,
    }
    nb = inputs["x"].shape[0]
    maps = []
    for b in range(nb):
        m = dict(shared)
        m["x"] = np.ascontiguousarray(inputs["x"][b], dtype=np.float32)
        m["mem"] = np.ascontiguousarray(inputs["mem"][b], dtype=np.float32)
        maps.append(m)
    return maps


def kernel(**inputs):
    SL = inputs["x"].shape[1]
    nb = inputs["x"].shape[0]
    nc = build_nc(SL)
    maps = make_in_maps(inputs, SL)
    res = run_bass_kernel_spmd(nc, maps, core_ids=list(range(nb)))
    return np.stack([np.asarray(r["out"], dtype=np.float32) for r in res.results], 0)
```
